# Optimizing a Trainium2 kernel written in Bass

```python
import jax, jax.numpy as jnp
from jax import lax
import numpy as np

D_MODEL = 2048
BATCH = 4
SEQ = 2048
DEPTH = 1

RMS_EPS = 1e-6
LN_EPS = 1e-5

RWKV_WIDTH = D_MODEL // 2
RWKV_HEAD = 64
RWKV_HEADS = RWKV_WIDTH // RWKV_HEAD
LNX_EPS = 64e-5


def _lora_dim(factor, power):
    return max(32, int(round(factor * D_MODEL ** power / 32)) * 32)


DECAY_LORA = _lora_dim(1.8, 0.5)
A_LORA = _lora_dim(1.8, 0.5)
GATE_LORA = _lora_dim(0.6, 0.8)

SGU_WIDTH = D_MODEL // 2
CHUNK = 128
SGU_GROUP_DIM = 128
SGU_GROUPS = SGU_WIDTH // SGU_GROUP_DIM

FFN_HIDDEN = ((8 * D_MODEL // 3 + 255) // 256) * 256

RWKV_COLS = 3 * RWKV_WIDTH + DECAY_LORA + A_LORA + GATE_LORA
SGU_COLS = 2 * SGU_WIDTH
GATE_COLS = 2 * D_MODEL
IN_COLS = RWKV_COLS + SGU_COLS + GATE_COLS
IN_SPLITS = [RWKV_COLS, RWKV_COLS + SGU_COLS, RWKV_COLS + SGU_COLS + D_MODEL]
RWKV_SPLITS = [RWKV_WIDTH, 2 * RWKV_WIDTH, 3 * RWKV_WIDTH,
               3 * RWKV_WIDTH + DECAY_LORA, 3 * RWKV_WIDTH + DECAY_LORA + A_LORA]

kernel_name = "rwkv7_sgu_gated_hybrid_block"


def _rmsnorm(x, g):
    xf = x.astype(jnp.float32)
    y = xf * lax.rsqrt(jnp.mean(xf * xf, axis=-1, keepdims=True) + RMS_EPS)
    return (y * g.astype(jnp.float32)).astype(x.dtype)


def _token_shift(p, mu):
    prev = jnp.pad(p, ((0, 0), (1, 0), (0, 0)))[:, :-1]
    return p + (prev - p) * mu


def _wkv7(r, decay, k, v, a, b):
    Bb, T, H, N = r.shape

    def step(S, inp):
        r_t, w_t, k_t, v_t, a_t, b_t = inp
        sa = jnp.einsum('bhij,bhj->bhi', S, a_t)
        S = (S * w_t[:, :, None, :] + sa[..., None] * b_t[:, :, None, :]
             + v_t[..., None] * k_t[:, :, None, :])
        y_t = jnp.einsum('bhij,bhj->bhi', S, r_t)
        return S, y_t

    xs = tuple(jnp.moveaxis(t, 1, 0) for t in (r, decay, k, v, a, b))
    S0 = jnp.zeros((Bb, H, N, N), jnp.float32)
    _, y = lax.scan(step, S0, xs)
    return jnp.moveaxis(y, 0, 1)


def _rwkv7_time_mix(p, mu, w0, w_lora_up, a0, a_lora_up, g_lora_up, k_k, k_a, r_k, lnx_g, lnx_b):
    Bb, T, _ = p.shape
    H, N = RWKV_HEADS, RWKV_HEAD
    f32 = jnp.float32
    p = _token_shift(p, mu)
    r, k, v, xw, xa, xg = jnp.split(p, RWKV_SPLITS, axis=-1)
    w_log = -jax.nn.softplus(-(w0 + jnp.tanh(xw) @ w_lora_up)) - 0.5
    decay = jnp.exp(-jnp.exp(w_log.astype(f32)))
    a = jax.nn.sigmoid(a0 + xa @ a_lora_up)
    g = jax.nn.sigmoid(xg) @ g_lora_up

    heads = lambda t: t.astype(f32).reshape(Bb, T, H, N)
    kk = heads(k * k_k)
    kk = kk / jnp.maximum(jnp.sqrt(jnp.sum(kk * kk, axis=-1, keepdims=True)), 1e-12)
    k = k * (1.0 + (a - 1.0) * k_a)
    r_h, k_h, v_h, a_h, w_h = heads(r), heads(k), heads(v), heads(a), heads(decay)

    y = _wkv7(r_h, w_h, k_h, v_h, -kk, kk * a_h)
    mean = jnp.mean(y, axis=-1, keepdims=True)
    var = jnp.mean(jnp.square(y - mean), axis=-1, keepdims=True)
    y = ((y - mean) * lax.rsqrt(var + LNX_EPS)).reshape(Bb, T, RWKV_WIDTH)
    y = y * lnx_g.astype(f32) + lnx_b.astype(f32)
    bonus = jnp.sum(r_h * k_h * r_k.astype(f32), axis=-1, keepdims=True) * v_h
    y = y + bonus.reshape(Bb, T, RWKV_WIDTH)
    return (y * g.astype(f32)).astype(p.dtype)


def _chunked_sgu(z, ln_g, ln_b, w_s, b_s):
    Bb, T, _ = z.shape
    z = jax.nn.gelu(z, approximate=False)
    u, v = jnp.split(z, 2, axis=-1)
    vf = v.astype(jnp.float32)
    mean = jnp.mean(vf, axis=-1, keepdims=True)
    var = jnp.mean(jnp.square(vf - mean), axis=-1, keepdims=True)
    v = (((vf - mean) * lax.rsqrt(var + LN_EPS)) * ln_g.astype(jnp.float32)
         + ln_b.astype(jnp.float32)).astype(z.dtype)
    vc = v.reshape(Bb, T // CHUNK, CHUNK, SGU_GROUPS, SGU_GROUP_DIM)
    w_causal = w_s * jnp.tril(jnp.ones((CHUNK, CHUNK), w_s.dtype))
    mixed = jnp.einsum('gts,bnsgc->bntgc', w_causal, vc) + b_s.T[None, None, :, :, None]
    return u * mixed.reshape(Bb, T, SGU_WIDTH)


def setup_inputs(seed: int = 0) -> dict:
    key = jax.random.key(seed)
    ks = iter(jax.random.split(key, 32))
    nrm = lambda shape, scale: jax.random.normal(next(ks), shape, jnp.float32) * scale
    L, RW = DEPTH, RWKV_WIDTH
    return {
        "x": nrm((BATCH, SEQ, D_MODEL), 1.0),
        "norm_mix_g": 1.0 + nrm((L, D_MODEL), 0.02),
        "w_in": nrm((L, D_MODEL, IN_COLS), D_MODEL ** -0.5),
        "shift_mu": jax.random.uniform(next(ks), (L, RWKV_COLS), jnp.float32),
        "w0": -2.0 + nrm((L, RW), 1.0),
        "w_lora_up": nrm((L, DECAY_LORA, RW), DECAY_LORA ** -0.5),
        "a0": nrm((L, RW), 0.5),
        "a_lora_up": nrm((L, A_LORA, RW), A_LORA ** -0.5),
        "g_lora_up": nrm((L, GATE_LORA, RW), GATE_LORA ** -0.5),
        "k_k": 1.0 + nrm((L, RW), 0.1),
        "k_a": 1.0 + nrm((L, RW), 0.1),
        "r_k": nrm((L, RWKV_HEADS, RWKV_HEAD), 0.1),
        "lnx_g": 1.0 + nrm((L, RW), 0.02),
        "lnx_b": nrm((L, RW), 0.02),
        "w_proj_rwkv": nrm((L, RW, D_MODEL), RW ** -0.5),
        "sgu_ln_g": 1.0 + nrm((L, SGU_WIDTH), 0.02),
        "sgu_ln_b": nrm((L, SGU_WIDTH), 0.02),
        "sgu_w": nrm((L, SGU_GROUPS, CHUNK, CHUNK), CHUNK ** -0.5),
        "sgu_b": 1.0 + nrm((L, SGU_GROUPS, CHUNK), 0.02),
        "w_proj_sgu": nrm((L, SGU_WIDTH, D_MODEL), SGU_WIDTH ** -0.5),
        "w_out": nrm((L, D_MODEL, D_MODEL), D_MODEL ** -0.5),
        "norm_ffn_g": 1.0 + nrm((L, D_MODEL), 0.02),
        "w_ffn_gate": nrm((L, D_MODEL, FFN_HIDDEN), D_MODEL ** -0.5),
        "w_ffn_up": nrm((L, D_MODEL, FFN_HIDDEN), D_MODEL ** -0.5),
        "w_ffn_down": nrm((L, FFN_HIDDEN, D_MODEL), FFN_HIDDEN ** -0.5),
        "norm_final_g": 1.0 + nrm((D_MODEL,), 0.02),
    }


def reference(x, norm_mix_g, w_in, shift_mu, w0, w_lora_up, a0, a_lora_up, g_lora_up,
              k_k, k_a, r_k, lnx_g, lnx_b, w_proj_rwkv, sgu_ln_g, sgu_ln_b, sgu_w, sgu_b,
              w_proj_sgu, w_out, norm_ffn_g, w_ffn_gate, w_ffn_up, w_ffn_down, norm_final_g):
    h = x
    for l in range(DEPTH):
        n = _rmsnorm(h, norm_mix_g[l])
        proj = n @ w_in[l]
        p_rwkv, z_sgu, gate_a, gate_b = jnp.split(proj, IN_SPLITS, axis=-1)
        y_a = _rwkv7_time_mix(p_rwkv, shift_mu[l], w0[l], w_lora_up[l], a0[l], a_lora_up[l],
                              g_lora_up[l], k_k[l], k_a[l], r_k[l], lnx_g[l], lnx_b[l])
        y_b = _chunked_sgu(z_sgu, sgu_ln_g[l], sgu_ln_b[l], sgu_w[l], sgu_b[l])
        merged = (jax.nn.sigmoid(gate_a) * (y_a @ w_proj_rwkv[l])
                  + jax.nn.sigmoid(gate_b) * (y_b @ w_proj_sgu[l]))
        h = h + merged @ w_out[l]
        n = _rmsnorm(h, norm_ffn_g[l])
        h = h + (jax.nn.silu(n @ w_ffn_gate[l]) * (n @ w_ffn_up[l])) @ w_ffn_down[l]
    return _rmsnorm(h, norm_final_g)
```

```python
import math
from contextlib import ExitStack

import numpy as np
import concourse.bass as bass
import concourse.mybir as mybir
from concourse.bass_utils import run_bass_kernel_spmd

F32 = mybir.dt.float32
BF16 = mybir.dt.bfloat16
AF = mybir.ActivationFunctionType
ALU = mybir.AluOpType
AX = mybir.AxisListType

P = 128
D = 2048
KD = 16
TB = 512
NTL = 4
RW = 1024
NH = 16
FH = 5632
NBLK = 4
OWN0 = 2
DEC = -math.exp(-0.5)
RMS_EPS = 1e-6
LN_EPS = 1e-5
LNX_EPS = 64e-5
SLOT = 6144
NSLOT = 4
PQ_RATIO = 1.5

PV = {}
_c = 0
for _n, _w in (("gmix", 16), ("gffn", 16), ("gfin", 16), ("mu", 28), ("w0", 8), ("a0", 8), ("kk", 8),
               ("ka", 8), ("rk", 8), ("lng", 8), ("lnb", 8)):
    PV[_n] = _c
    _c += _w
NPV = _c


def weight_layout():
    lay = {}
    off = 0

    def add(name, kc, nc_):
        nonlocal off
        lay[name] = (off, kc, nc_)
        off += kc * nc_

    for hp in range(8):
        add(f"kvr{hp}", 16, 384)
    add("lora_wa", 16, 192)
    add("lora_g", 16, 256)
    for i in range(4):
        add(f"su{i}", 16, 256)
    for i in range(4):
        add(f"sv{i}", 16, 256)
    for cg in range(8):
        add(f"ga{cg}", 16, 256)
        add(f"gb{cg}", 16, 256)
        add(f"pj{cg}", 16, 256)
    for cg in range(8):
        add(f"wo{cg}", 16, 256)
    for j in range(44):
        add(f"gu{j}", 16, 256)
    for hf in range(2):
        for sp in range(8):
            add(f"dn{hf}_{sp}", 22, 256)
    return lay, off


LAY, WTOT = weight_layout()


def _tile_w(Wm, c0, c1):
    K = Wm.shape[0]
    kc = K // P
    return Wm[:, c0:c1].reshape(kc, P, c1 - c0).transpose(1, 0, 2)


def pack_weights(w_in, w_proj_rwkv, w_proj_sgu, w_out, w_ffn_gate, w_ffn_up, w_ffn_down):
    wst = np.empty((P, WTOT), np.float32)

    def put(name, arr):
        off, kc, nc_ = LAY[name]
        assert arr.shape == (P, kc, nc_), (name, arr.shape)
        wst[:, off:off + kc * nc_] = arr.reshape(P, kc * nc_)

    for hp in range(8):
        a = np.concatenate([_tile_w(w_in, 1024 + hp * 128, 1024 + hp * 128 + 128),
                            _tile_w(w_in, 2048 + hp * 128, 2048 + hp * 128 + 128),
                            _tile_w(w_in, hp * 128, hp * 128 + 128)], axis=2)
        put(f"kvr{hp}", a)
    put("lora_wa", _tile_w(w_in, 3072, 3264))
    put("lora_g", _tile_w(w_in, 3264, 3520))
    for i in range(4):
        put(f"su{i}", _tile_w(w_in, 3520 + i * 256, 3520 + (i + 1) * 256))
        put(f"sv{i}", _tile_w(w_in, 4544 + i * 256, 4544 + (i + 1) * 256))
    for cg in range(8):
        put(f"ga{cg}", _tile_w(w_in, 5568 + cg * 256, 5568 + (cg + 1) * 256))
        put(f"gb{cg}", _tile_w(w_in, 7616 + cg * 256, 7616 + (cg + 1) * 256))
        a = np.concatenate([_tile_w(w_proj_rwkv, cg * 256, (cg + 1) * 256),
                            _tile_w(w_proj_sgu, cg * 256, (cg + 1) * 256)], axis=1)
        put(f"pj{cg}", a)
        put(f"wo{cg}", _tile_w(w_out, cg * 256, (cg + 1) * 256))
    for j in range(44):
        a = np.concatenate([_tile_w(w_ffn_gate, j * 128, (j + 1) * 128),
                            _tile_w(w_ffn_up, j * 128, (j + 1) * 128)], axis=2)
        put(f"gu{j}", a)
    for hf in range(2):
        for sp in range(8):
            put(f"dn{hf}_{sp}", _tile_w(w_ffn_down[hf * 2816:(hf + 1) * 2816], sp * 256, (sp + 1) * 256))
    return wst


class Res:
    __slots__ = ("writer", "readers", "excl")

    def __init__(self, excl=False):
        self.writer = None
        self.readers = {}
        self.excl = excl


class T:
    def __init__(self, h, excl=False):
        self.h = h
        self.ap = h[:]
        self.res = Res(excl)


class Eng:
    def __init__(self, name, h, sem):
        self.name = name
        self.h = h
        self.sem = sem
        self.count = 0
        self.seen = {}


class DSem:
    def __init__(self, key, sem):
        self.key = key
        self.sem = sem
        self.count = 0


class KB:
    def __init__(self, nc, es):
        self.nc = nc
        self.es = es
        self.uid = 0
        self.engs = {}
        for name, h in (("pe", nc.tensor), ("act", nc.scalar), ("dve", nc.vector), ("pool", nc.gpsimd),
                        ("sp", nc.sync)):
            sem = es.enter_context(nc.semaphore(f"sem_{name}"))
            self.engs[name] = Eng(name, h, sem)
        self.dsems = []
        self.banks = [T(es.enter_context(nc.psum_tensor(f"bank{i}", [P, 512], F32)), excl=True) for i in range(8)]
        self.bank_i = 0
        self.reserved = []
        self.sb_ptr = 16640
        self.arena_base = None

    def alloc(self, shape, dtype):
        n = 1
        for s in shape[1:]:
            n *= s
        nbytes = n * (4 if dtype == F32 else 2)
        nbytes = (nbytes + 63) // 64 * 64
        self.uid += 1
        h = self.nc.alloc_sbuf_tensor_at(f"t{self.uid}", list(shape), dtype, offset=self.sb_ptr)
        self.sb_ptr += nbytes
        assert self.sb_ptr <= 229000, self.sb_ptr
        return T(h)

    def arena_reset(self):
        self.barrier()
        self.sb_ptr = self.arena_base

    def new_dsem(self, name):
        self.uid += 1
        d = DSem(f"d{self.uid}", self.es.enter_context(self.nc.semaphore(f"ds_{name}_{self.uid}")))
        self.dsems.append(d)
        return d

    def bank(self, reserve=False):
        while True:
            b = self.banks[self.bank_i]
            self.bank_i = (self.bank_i + 1) % 8
            if b not in self.reserved:
                break
        if reserve:
            self.reserved.append(b)
        return b

    def _wait(self, eng, m, raw):
        key, sem, val = m
        if key == eng.name:
            if eng.name == "pe":
                return
        if eng.seen.get(key, 0) >= val:
            return
        eng.h.wait_ge(sem, val)
        eng.seen[key] = val

    def _pre(self, eng, reads, writes):
        for t in reads:
            r = t.res
            if r.writer is not None:
                self._wait(eng, r.writer, True)
            if r.excl:
                for m in r.readers.values():
                    self._wait(eng, m, False)
        for t in writes:
            r = t.res
            if r.writer is not None:
                self._wait(eng, r.writer, True)
            for m in r.readers.values():
                self._wait(eng, m, False)

    def _post(self, key, mk, reads, writes):
        for t in reads:
            r = t.res
            if r.excl:
                r.writer = mk
                r.readers = {}
            else:
                r.readers[key] = mk
        for t in writes:
            r = t.res
            r.writer = mk
            r.readers = {}

    def op(self, en, fn, reads=(), writes=(), inc=True):
        eng = self.engs[en]
        self._pre(eng, reads, writes)
        ins = fn(eng.h)
        if inc:
            ins.then_inc(eng.sem, 1)
            eng.count += 1
            mk = (en, eng.sem, eng.count)
        else:
            mk = (en, eng.sem, eng.count + 1)
        self._post(en, mk, reads, writes)

    def dma(self, qn, out, in_, reads=(), writes=(), ds=None):
        eng = self.engs[qn]
        self._pre(eng, reads, writes)
        eng.h.dma_start(out=out, in_=in_).then_inc(ds.sem, 16)
        ds.count += 16
        mk = (ds.key, ds.sem, ds.count)
        self._post(ds.key, mk, reads, writes)

    def barrier(self):
        for en in ("pe", "act", "dve", "sp"):
            eng = self.engs[en]
            for on in ("pe", "act", "dve"):
                o = self.engs[on]
                if o.count > 0:
                    self._wait(eng, (on, o.sem, o.count), True)
            for ds in self.dsems:
                if ds.count > 0:
                    self._wait(eng, (ds.key, ds.sem, ds.count), True)

    def mm(self, bank, out, lhsT, rhs, start, stop, reads, inc=None):
        self.op("pe", lambda e: e.matmul(out, lhsT=lhsT, rhs=rhs, start=start, stop=stop), reads=reads,
                writes=[bank], inc=stop if inc is None else inc)

    def tr(self, bank, out, in_, ident, reads, inc):
        self.op("pe", lambda e: e.transpose(out, in_, ident), reads=reads, writes=[bank], inc=inc)

    def act(self, out, in_, func, reads, writes, bias=None, scale=None):
        kw = {}
        if bias is not None:
            kw["bias"] = bias
        if scale is not None:
            kw["scale"] = scale
        self.op("act", lambda e: e.activation(out=out, in_=in_, func=func, **kw), reads=reads, writes=writes)

    def tt(self, en, out, in0, in1, op, reads, writes):
        self.op(en, lambda e: e.tensor_tensor(out=out, in0=in0, in1=in1, op=op), reads=reads, writes=writes)

    def ts(self, en, out, in0, s1, s2, op0, op1, reads, writes):
        if s2 is None:
            self.op(en, lambda e: e.tensor_scalar(out=out, in0=in0, scalar1=s1, scalar2=None, op0=op0),
                    reads=reads, writes=writes)
        else:
            self.op(en, lambda e: e.tensor_scalar(out=out, in0=in0, scalar1=s1, scalar2=s2, op0=op0, op1=op1),
                    reads=reads, writes=writes)

    def stt(self, en, out, in0, scalar, in1, op0, op1, reads, writes):
        self.op(en, lambda e: e.scalar_tensor_tensor(out=out, in0=in0, scalar=scalar, in1=in1, op0=op0, op1=op1),
                reads=reads, writes=writes)

    def cp(self, en, out, in_, reads, writes):
        if en == "act":
            self.act(out, in_, AF.Copy, reads, writes)
        else:
            self.op(en, lambda e: e.tensor_copy(out=out, in_=in_), reads=reads, writes=writes)


def v3(ap, b=128):
    return ap.rearrange("p (a b) -> p a b", b=b)


def build_nc(debug=None):
    nc = bass.Bass("TRN2", target_bir_lowering=False)
    xs = nc.dram_tensor("xs", [2048, D], F32, kind="ExternalInput").ap()
    wst = nc.dram_tensor("wst", [P, WTOT], F32, kind="ExternalInput").ap()
    pvec_d = nc.dram_tensor("pvec", [P, NPV], F32, kind="ExternalInput").ap()
    bvec_d = nc.dram_tensor("bvec", [P, 2048], F32, kind="ExternalInput").ap()
    sguw_d = nc.dram_tensor("sguw", [P, 1024], F32, kind="ExternalInput").ap()
    sgub_d = nc.dram_tensor("sgub", [1, 1024], F32, kind="ExternalInput").ap()
    lora_d = nc.dram_tensor("lora", [P, 4096], F32, kind="ExternalInput").ap()
    out_d = nc.dram_tensor("out", [1024, D], F32, kind="ExternalOutput").ap()
    dbg_d = {}
    if debug:
        for name, shape in debug.items():
            dbg_d[name] = nc.dram_tensor("dbg_" + name, list(shape), F32, kind="ExternalOutput").ap()

    with ExitStack() as es:
        k = KB(nc, es)
        A = k.alloc
        ident_f = A([P, P], F32)
        ident_b = A([P, P], BF16)
        blockones = A([P, P], F32)
        ones_b = A([P, P], BF16)
        ones_row = A([1, P], F32)
        mask512 = A([P, 512], F32)
        maskL = A([P, 512], F32)
        resetm = A([P, 512], F32)
        pvec = A([P, NPV], F32)
        omm = A([P, 28], F32)
        omka = A([P, 8], F32)
        bvec = A([P, 2048], F32)
        sguw_f = A([P, 1024], F32)
        sguw_b = A([P, 1024], BF16)
        sgub = A([1, 1024], F32)
        wlu = A([P, 1024], BF16)
        alu = A([P, 1024], BF16)
        glu = A([P, 2048], BF16)
        pprev = A([P, 28], F32)
        S32 = [A([P, 64], F32) for _ in range(NH)]
        Sbf = [A([P, 64], BF16) for _ in range(NH)]
        ring = [A([P, SLOT], BF16) for _ in range(NSLOT)]
        ring_ds = [k.new_dsem(f"ring{i}") for i in range(NSLOT)]
        k.dsems = []
        n_fm = A([P, KD, TB], BF16)
        ya_fm = A([P, 8, TB], BF16)
        yb_fm = A([P, 8, TB], BF16)
        k.arena_base = k.sb_ptr
        ring_i = [0]
        setup_ds = k.new_dsem("setup")
        setup_ds2 = k.new_dsem("setup2")
        x_ds = [k.new_dsem("x0"), k.new_dsem("x1")]
        o_ds = [k.new_dsem("o0"), k.new_dsem("o1")]
        dbg_ds = k.new_dsem("dbg")

        def pv(name, j=0, rows=P):
            c = PV[name] + j
            return pvec.ap[0:rows, c:c + 1]

        def wload(name, csub=None):
            off, kc, ncol = LAY[name]
            i = ring_i[0]
            ring_i[0] = (i + 1) % NSLOT
            slot = ring[i]
            if csub is None:
                k.dma("pool", slot.ap[:, 0:kc * ncol], wst[:, off:off + kc * ncol], writes=[slot], ds=ring_ds[i])
            else:
                src = wst[:, off:off + kc * ncol].rearrange("p (k n) -> p k n", n=ncol)[:, :, 0:csub]
                dst = slot.ap[:, 0:kc * ncol].rearrange("p (k n) -> p k n", n=ncol)[:, :, 0:csub]
                k.dma("pool", dst, src, writes=[slot], ds=ring_ds[i])
            return slot, slot.ap[:, 0:kc * ncol].rearrange("p (k n) -> p k n", n=ncol)

        def dbg_dump(name, t, ap=None):
            if name in dbg_d:
                k.dma("pool", dbg_d[name], t.ap if ap is None else ap, reads=[t], ds=dbg_ds)

        k.op("pool", lambda e: e.memset(ident_f.ap, 0.0), writes=[ident_f])
        k.op("pool", lambda e: e.affine_select(out=ident_f.ap, in_=ident_f.ap, pattern=[[-1, P]],
                                                 compare_op=ALU.not_equal, fill=1.0, base=0, channel_multiplier=1),
             reads=[ident_f], writes=[ident_f])
        k.cp("dve", ident_b.ap, ident_f.ap, [ident_f], [ident_b])
        k.op("dve", lambda e: e.memset(blockones.ap, 0.0), writes=[blockones])
        k.op("dve", lambda e: e.memset(blockones.ap[0:64, 0:64], 1.0), writes=[blockones])
        k.op("dve", lambda e: e.memset(blockones.ap[64:128, 64:128], 1.0), writes=[blockones])
        k.op("dve", lambda e: e.memset(ones_b.ap, 1.0), writes=[ones_b])
        k.op("dve", lambda e: e.memset(ones_row.ap, 1.0), writes=[ones_row])
        k.op("dve", lambda e: e.memset(pprev.ap, 0.0), writes=[pprev])
        for h in range(NH):
            k.op("dve", lambda e, h=h: e.memset(S32[h].ap, 0.0), writes=[S32[h]])
            k.op("dve", lambda e, h=h: e.memset(Sbf[h].ap, 0.0), writes=[Sbf[h]])
        k.op("pool", lambda e: e.memset(mask512.ap, 1.0), writes=[mask512])
        k.op("pool", lambda e: e.memset(maskL.ap, 1.0), writes=[maskL])
        for u in range(4):
            sl = mask512.ap[:, u * 128:(u + 1) * 128]
            cmpop = ALU.is_gt if u % 2 == 0 else ALU.is_ge
            k.op("pool", lambda e, sl=sl, cmpop=cmpop: e.affine_select(
                out=sl, in_=sl, pattern=[[1, P]], compare_op=cmpop, fill=0.0, base=0, channel_multiplier=-1),
                reads=[mask512], writes=[mask512])
            sl2 = maskL.ap[:, u * 128:(u + 1) * 128]
            k.op("pool", lambda e, sl2=sl2: e.affine_select(
                out=sl2, in_=sl2, pattern=[[-1, P]], compare_op=ALU.is_gt, fill=0.0, base=0, channel_multiplier=1),
                reads=[maskL], writes=[maskL])
        k.op("dve", lambda e: e.memset(resetm.ap, 1.0), writes=[resetm])
        k.op("dve", lambda e: e.memset(v3(resetm.ap)[:, :, 0:1], 0.0), writes=[resetm])
        k.dma("sp", pvec.ap, pvec_d, writes=[pvec], ds=setup_ds)
        k.dma("sp", bvec.ap, bvec_d, writes=[bvec], ds=setup_ds)
        k.dma("sp", sguw_f.ap, sguw_d, writes=[sguw_f], ds=setup_ds)
        k.dma("sp", sgub.ap, sgub_d, writes=[sgub], ds=setup_ds)
        k.dma("pool", wlu.ap[0:96, :], lora_d[0:96, 0:1024], writes=[wlu], ds=setup_ds2)
        k.dma("pool", alu.ap[0:96, :], lora_d[0:96, 1024:2048], writes=[alu], ds=setup_ds2)
        k.dma("pool", glu.ap, lora_d[:, 2048:4096], writes=[glu], ds=setup_ds2)
        _fin = (setup_ds.key, setup_ds.sem, setup_ds.count)
        for _t in (pvec, bvec, sguw_f, sgub):
            _t.res.writer = _fin
        _fin2 = (setup_ds2.key, setup_ds2.sem, setup_ds2.count)
        for _t in (wlu, alu, glu):
            _t.res.writer = _fin2
        mu0 = PV["mu"]
        k.ts("dve", omm.ap, pvec.ap[:, mu0:mu0 + 28], -1.0, 1.0, ALU.mult, ALU.add, [pvec], [omm])
        k.ts("dve", omka.ap, pvec.ap[:, PV["ka"]:PV["ka"] + 8], -1.0, 1.0, ALU.mult, ALU.add, [pvec], [omka])
        k.tt("dve", v3(sguw_b.ap), v3(sguw_f.ap),
             mask512.ap[:, 128:256].unsqueeze(1).to_broadcast([P, 8, 128]), ALU.mult, [sguw_f, mask512], [sguw_b])

        def load_x_fm(blk, h_fm, xt):
            for tl in range(NTL):
                xtt = xt[tl % 2]
                r0 = blk * TB + tl * 128
                k.dma("sp", xtt.ap, xs[r0:r0 + 128, :], writes=[xtt], ds=x_ds[tl % 2])
                for q in range(4):
                    b = k.bank()
                    for j in range(4):
                        kc = q * 4 + j
                        k.tr(b, b.ap[:, j * 128:(j + 1) * 128], xtt.ap[:, kc * 128:(kc + 1) * 128], ident_f.ap,
                             [xtt, ident_f], inc=(j == 3))
                    k.cp("act" if q % 2 else "dve", h_fm.ap[:, q * 4:(q + 1) * 4, tl * 128:(tl + 1) * 128],
                         v3(b.ap), [b], [h_fm])

        def rmsnorm_fm(h_fm, gname, dst, sq, rstd, dst_f32=False):
            b = k.bank()
            for kc in range(KD):
                s = sq[kc % 2]
                k.act(s.ap, h_fm.ap[:, kc, :], AF.Square, [h_fm], [s])
                k.mm(b, b.ap, ones_b.ap, s.ap, kc == 0, kc == KD - 1, [ones_b, s], inc=True)
            k.act(rstd.ap, b.ap, AF.Sqrt, [b, eps_rms], [rstd], bias=eps_rms.ap[:, 0:1], scale=1.0 / D)
            k.op("dve", lambda e: e.reciprocal(out=rstd.ap, in_=rstd.ap), reads=[rstd], writes=[rstd])
            for kc in range(KD):
                k.stt("dve", dst.ap[:, kc, :], h_fm.ap[:, kc, :], pv(gname, kc), rstd.ap, ALU.mult, ALU.mult,
                      [h_fm, pvec, rstd], [dst])

        eps_rms = A([P, 4], F32)
        k.arena_base = k.sb_ptr
        k.op("dve", lambda e: e.memset(eps_rms.ap[:, 0:1], RMS_EPS), writes=[eps_rms])
        k.op("dve", lambda e: e.memset(eps_rms.ap[:, 1:2], LN_EPS), writes=[eps_rms])
        k.op("dve", lambda e: e.memset(eps_rms.ap[:, 2:3], LNX_EPS), writes=[eps_rms])

        def tshift(b, rows, c, dst):
            mu = pvec.ap[0:rows, mu0 + c:mu0 + c + 1]
            k.act(dst.ap[0:rows, :], b.ap[0:rows, :], AF.Identity, [b, omm], [dst], scale=omm.ap[0:rows, c:c + 1])
            k.stt("dve", dst.ap[0:rows, 1:TB], b.ap[0:rows, 0:TB - 1], mu, dst.ap[0:rows, 1:TB], ALU.mult, ALU.add,
                  [b, pvec, dst], [dst])
            k.stt("dve", dst.ap[0:rows, 0:1], pprev.ap[0:rows, c:c + 1], mu, dst.ap[0:rows, 0:1], ALU.mult, ALU.add,
                  [pprev, pvec, dst], [dst])
            k.cp("act", pprev.ap[0:rows, c:c + 1], b.ap[0:rows, TB - 1:TB], [b], [pprev])

        def proj_fm(b, wv, c0, ncol, rows=None):
            slot, view = wv
            for kc in range(KD):
                k.mm(b, b.ap[0:ncol, :], view[:, kc, c0:c0 + ncol], n_fm.ap[:, kc, :], kc == 0, kc == KD - 1,
                     [slot, n_fm], inc=(kc == KD - 1))

        for blk in range(NBLK):
            own = blk >= OWN0
            do_r = blk >= OWN0 - 1
            h0 = A([P, KD, TB], F32)
            xt = [A([P, D], F32), A([P, D], F32)]
            sq = [A([P, TB], BF16), A([P, TB], BF16)]
            rstd = A([P, TB], F32)
            load_x_fm(blk, h0, xt)
            rmsnorm_fm(h0, "gmix", n_fm, sq, rstd)
            if blk == OWN0:
                dbg_dump("n_fm", n_fm)
            k.arena_reset()

            txw = A([P, TB], BF16)
            xab = A([P, TB], BF16)
            sxg = A([P, 2, TB], BF16)
            tmpf = A([P, TB], F32)
            wv = wload("lora_wa")
            b = k.bank()
            proj_fm(b, wv, 0, 96)
            tshift(b, 96, 24, tmpf)
            k.act(txw.ap[0:96, :], tmpf.ap[0:96, :], AF.Tanh, [tmpf], [txw])
            b = k.bank()
            proj_fm(b, wv, 96, 96)
            tshift(b, 96, 25, tmpf)
            k.cp("dve", xab.ap[0:96, :], tmpf.ap[0:96, :], [tmpf], [xab])
            if do_r:
                wv = wload("lora_g")
                for j in range(2):
                    b = k.bank()
                    proj_fm(b, wv, j * 128, 128)
                    tshift(b, 128, 26 + j, tmpf)
                    k.act(sxg.ap[:, j, :], tmpf.ap, AF.Sigmoid, [tmpf], [sxg])

            k_f = A([P, TB], F32); v_f = A([P, TB], F32); r_f = A([P, TB], F32); a_f = A([P, TB], F32)
            bA = A([P, TB], F32); bB = A([P, TB], F32); kkr = A([P, TB], F32)
            bD = A([P, TB], F32); bE = A([P, TB], F32); bF = A([P, TB], F32)
            bh_b = A([P, TB], BF16); kh_b = A([P, TB], BF16); v_b = A([P, TB], BF16)
            A1 = [A([P, NTL, 256], BF16) for _ in range(2)]
            A2 = [A([P, NTL, 256], BF16) for _ in range(2)]
            Xb = [[A([P, NTL, 128], BF16) for _ in range(2)] for _ in range(2)]
            Yb = [[A([P, NTL, 128], BF16) for _ in range(2)] for _ in range(2)]
            Tb = [[A([P, NTL, 128], BF16) for _ in range(2)] for _ in range(2)]
            Xs = [A([P, 64], BF16) for _ in range(2)]
            Us = [A([P, 64], BF16) for _ in range(2)]
            ypair = A([P, TB], F32); ysq = A([P, TB], F32)
            st8 = A([P, 8, 8], F32)
            t1 = A([P, TB], F32)

            def alloc_set():
                bC = A([P, TB], F32); bG = A([P, TB], F32)
                AR = A([P, NTL, 2, 128], BF16)
                bt_b = A([P, TB], BF16); kt_b = A([P, TB], BF16)
                TMav = A([P, 2, NTL, 128], BF16)
                TMbk = A([P, 2, NTL, 128], BF16)
                return (bC, bG, AR, bt_b, kt_b, TMav, TMbk)

            sets = [alloc_set(), alloc_set()]
            if not own:
                for _s in sets:
                    k.op("dve", lambda e, _s=_s: e.memset(_s[2].ap, 0.0), writes=[_s[2]])

            def pair_gen(hp, bs):
                (bC, bG, AR, bt_b, kt_b, TMav, TMbk) = bs
                wv = wload(f"kvr{hp}", None if do_r else 256)
                for ci, dst in enumerate((k_f, v_f, r_f)):
                    if ci == 2 and not do_r:
                        continue
                    b = k.bank()
                    proj_fm(b, wv, ci * 128, 128)
                    tshift(b, 128, 3 * hp + ci, dst)
                    yield
                yield
                b = k.bank()
                k.mm(b, b.ap, wlu.ap[0:96, hp * 128:(hp + 1) * 128], txw.ap[0:96, :], True, True, [wlu, txw])
                yield
                k.act(bA.ap, b.ap, AF.Sigmoid, [b, pvec], [bA], bias=pv("w0", hp))
                yield
                b = k.bank()
                k.mm(b, b.ap, alu.ap[0:96, hp * 128:(hp + 1) * 128], xab.ap[0:96, :], True, True, [alu, xab])
                yield
                k.act(a_f.ap, b.ap, AF.Sigmoid, [b, pvec], [a_f], bias=pv("a0", hp))
                yield
                k.op("dve", lambda e: e.tensor_tensor_scan(out=bB.ap, data0=resetm.ap, data1=bA.ap, initial=0.0,
                                                           op0=ALU.mult, op1=ALU.add), reads=[resetm, bA], writes=[bB])
                k.tt("dve", bA.ap, bB.ap, bA.ap, ALU.subtract, [bB, bA], [bA])
                k.act(bC.ap, bB.ap, AF.Exp, [bB], [bC], scale=DEC)
                k.act(bB.ap, bB.ap, AF.Exp, [bB], [bB], scale=-DEC)
                k.act(bA.ap, bA.ap, AF.Exp, [bA], [bA], scale=DEC)
                yield
                k.ts("dve", kkr.ap, k_f.ap, pv("kk", hp), None, ALU.mult, None, [k_f, pvec], [kkr])
                yield
                k.act(bD.ap, kkr.ap, AF.Square, [kkr], [bD])
                yield
                b = k.bank()
                k.mm(b, b.ap, blockones.ap, bD.ap, True, True, [blockones, bD])
                yield
                k.act(bD.ap, b.ap, AF.Sqrt, [b], [bD])
                yield
                k.ts("dve", bD.ap, bD.ap, 1e-12, None, ALU.max, None, [bD], [bD])
                yield
                k.op("dve", lambda e: e.reciprocal(out=bD.ap, in_=bD.ap), reads=[bD], writes=[bD])
                yield
                k.tt("dve", kkr.ap, kkr.ap, bD.ap, ALU.mult, [kkr, bD], [kkr])
                yield
                k.ts("dve", bE.ap, a_f.ap, pv("ka", hp), omka.ap[:, hp:hp + 1], ALU.mult, ALU.add,
                     [a_f, pvec, omka], [bE])
                k.tt("dve", bE.ap, k_f.ap, bE.ap, ALU.mult, [k_f, bE], [bE])
                if own:
                    k.stt("dve", bG.ap, r_f.ap, pv("rk", hp), bE.ap, ALU.mult, ALU.mult, [r_f, pvec, bE], [bG])
                    yield
                    b = k.bank()
                    k.mm(b, b.ap, blockones.ap, bG.ap, True, True, [blockones, bG])
                    yield
                    k.tt("dve", bG.ap, b.ap, v_f.ap, ALU.mult, [b, v_f], [bG])
                    yield
                yield
                k.tt("dve", bF.ap, kkr.ap, a_f.ap, ALU.mult, [kkr, a_f], [bF])
                k.tt("dve", bF.ap, bF.ap, bB.ap, ALU.mult, [bF, bB], [bF])
                yield
                k.tt("dve", bE.ap, bE.ap, bB.ap, ALU.mult, [bE, bB], [bE])
                yield
                k.stt("dve", AR.ap[:, :, 0, :], v3(kkr.ap), -1.0, v3(bA.ap), ALU.mult, ALU.mult, [kkr, bA], [AR])
                yield
                if own:
                    k.tt("dve", AR.ap[:, :, 1, :], v3(r_f.ap), v3(bC.ap), ALU.mult, [r_f, bC], [AR])
                    yield
                yield
                k.cp("act", bt_b.ap, bF.ap, [bF], [bt_b])
                yield
                k.cp("act", kt_b.ap, bE.ap, [bE], [kt_b])
                yield
                pc = bC.ap[:, 127:TB:128]
                pcb = pc.unsqueeze(2).to_broadcast([P, NTL, 128])
                k.tt("dve", v3(bh_b.ap), v3(bF.ap), pcb, ALU.mult, [bF, bC], [bh_b])
                yield
                k.tt("dve", v3(kh_b.ap), v3(bE.ap), pcb, ALU.mult, [bE, bC], [kh_b])
                yield
                k.cp("act", v_b.ap, v_f.ap, [v_f], [v_b])
                yield
                yield
                for (dstT, srcs) in ((TMav, ((AR, lambda tl: AR.ap[:, tl, 0, :]),
                                              (v_b, lambda tl: v_b.ap[:, tl * 128:(tl + 1) * 128]))),
                                     (TMbk, ((bh_b, lambda tl: bh_b.ap[:, tl * 128:(tl + 1) * 128]),
                                             (kh_b, lambda tl: kh_b.ap[:, tl * 128:(tl + 1) * 128])))):
                    b = k.bank()
                    bb = b.ap.bitcast(BF16)
                    for qi, (srcT, fn) in enumerate(srcs):
                        for tl in range(NTL):
                            o = (qi * NTL + tl) * 128
                            k.tr(b, bb[:, o:o + 128], fn(tl), ident_b.ap, [srcT, ident_b], inc=(qi == 1 and tl == 3))
                    k.cp("act" if dstT is TMav else "dve", dstT.ap.rearrange("p a b c -> p (a b c)"), bb, [b], [dstT])
                    yield
                yield "Q"
                for hh in range(2):
                    R = slice(hh * 64, hh * 64 + 64)
                    for half in range(2):
                        b1 = k.bank()
                        b2 = k.bank()
                        for u in range(2):
                            tl = half * 2 + u
                            ts_ = slice(tl * 128, tl * 128 + 128)
                            rhs = AR.ap[R, tl, :, :].rearrange("p a b -> p (a b)")
                            k.mm(b1, b1.ap[:, u * 256:(u + 1) * 256], bt_b.ap[R, ts_], rhs, True, True, [bt_b, AR],
                                 inc=(u == 1))
                        for u in range(2):
                            tl = half * 2 + u
                            ts_ = slice(tl * 128, tl * 128 + 128)
                            rhs = AR.ap[R, tl, :, :].rearrange("p a b -> p (a b)")
                            k.mm(b2, b2.ap[:, u * 256:(u + 1) * 256], kt_b.ap[R, ts_], rhs, True, True, [kt_b, AR],
                                 inc=(u == 1))
                        k.tt("dve", A1[hh].ap[:, half * 2:half * 2 + 2, :].rearrange("p a b -> p (a b)"), b1.ap,
                             mask512.ap, ALU.mult, [b1, mask512], [A1[hh]])
                        k.tt("dve", A2[hh].ap[:, half * 2:half * 2 + 2, :].rearrange("p a b -> p (a b)"), b2.ap,
                             mask512.ap, ALU.mult, [b2, mask512], [A2[hh]])
                        yield
                    b3 = k.bank()
                    for tl in range(NTL):
                        ts_ = slice(tl * 128, tl * 128 + 128)
                        k.mm(b3, b3.ap[:, ts_], AR.ap[R, tl, 0, :], bt_b.ap[R, ts_], True, True, [AR, bt_b],
                             inc=(tl == 3))
                    k.tt("dve", Yb[hh][0].ap.rearrange("p a b -> p (a b)"), b3.ap, maskL.ap, ALU.mult, [b3, maskL],
                         [Yb[hh][0]])
                    k.cp("act", Xb[hh][0].ap, A1[hh].ap[:, :, 0:128], [A1[hh]], [Xb[hh][0]])
                    k.tt("dve", Tb[hh][0].ap, A1[hh].ap[:, :, 0:128],
                         ident_b.ap.unsqueeze(1).to_broadcast([P, NTL, 128]), ALU.add, [A1[hh], ident_b], [Tb[hh][0]])
                    yield
                cur = 0
                for lvl in range(1, 8):
                    nxt = 1 - cur
                    for hh in range(2):
                        Xc, Yc, Tc = Xb[hh][cur], Yb[hh][cur], Tb[hh][cur]
                        Xn, Yn, Tn = Xb[hh][nxt], Yb[hh][nxt], Tb[hh][nxt]
                        if lvl <= 6:
                            by = k.bank()
                            for tl in range(NTL):
                                k.mm(by, by.ap[:, tl * 128:(tl + 1) * 128], Xc.ap[:, tl, :], Yc.ap[:, tl, :], True, True,
                                     [Xc, Yc], inc=(tl == 3))
                            if lvl <= 5:
                                bx = k.bank()
                                for tl in range(NTL):
                                    k.mm(bx, bx.ap[:, tl * 128:(tl + 1) * 128], Yc.ap[:, tl, :], Xc.ap[:, tl, :], True,
                                         True, [Xc, Yc], inc=(tl == 3))
                        if lvl >= 2:
                            bt = k.bank()
                            for tl in range(NTL):
                                o = bt.ap[:, tl * 128:(tl + 1) * 128]
                                k.mm(bt, o, ident_b.ap, Tc.ap[:, tl, :], True, False, [ident_b, Tc])
                                k.mm(bt, o, Yc.ap[:, tl, :], Tc.ap[:, tl, :], False, True, [Yc, Tc], inc=(tl == 3))
                        if lvl <= 6:
                            k.cp("act", Yn.ap.rearrange("p a b -> p (a b)"), by.ap, [by], [Yn])
                            if lvl <= 5:
                                k.cp("dve", Xn.ap.rearrange("p a b -> p (a b)"), bx.ap, [bx], [Xn])
                        if lvl >= 2:
                            k.cp("dve" if hh else "act", Tn.ap.rearrange("p a b -> p (a b)"), bt.ap, [bt], [Tn])
                        else:
                            k.cp("dve", Tn.ap, Tc.ap, [Tc], [Tn])
                        yield
                    cur = nxt
                Tfin = [Tb[0][cur], Tb[1][cur]]
                yield "C"
                by_ = k.bank(reserve=True) if own else None
                for tl in range(NTL):
                    Rs = [slice(0, 64), slice(64, 128)]
                    hs = [hp * 2, hp * 2 + 1]
                    vt = [TMav.ap[:, 1, tl, Rs[hh]] for hh in range(2)]
                    bx = [k.bank(), k.bank()]
                    for hh in range(2):
                        R, h = Rs[hh], hs[hh]
                        k.mm(bx[hh], bx[hh].ap[:, 0:64], AR.ap[R, tl, 0, :], Sbf[h].ap[R, :], True, False,
                             [AR, Sbf[h]], inc=False)
                        k.mm(bx[hh], bx[hh].ap[:, 0:64], A2[hh].ap[:, tl, 0:128], vt[hh], False, True, [A2[hh], TMav])
                    yield
                    k.cp("act", Xs[0].ap, bx[0].ap[:, 0:64], [bx[0]], [Xs[0]])
                    k.cp("dve", Xs[1].ap, bx[1].ap[:, 0:64], [bx[1]], [Xs[1]])
                    yield
                    bu = [k.bank(), k.bank()]
                    for hh in range(2):
                        k.mm(bu[hh], bu[hh].ap[:, 0:64], Tfin[hh].ap[:, tl, :], Xs[hh].ap, True, True,
                             [Tfin[hh], Xs[hh]])
                    yield
                    k.cp("dve", Us[0].ap, bu[0].ap[:, 0:64], [bu[0]], [Us[0]])
                    k.cp("act", Us[1].ap, bu[1].ap[:, 0:64], [bu[1]], [Us[1]])
                    yield
                    bs = [k.bank(), k.bank()]
                    for hh in range(2):
                        R, h = Rs[hh], hs[hh]
                        k.mm(bs[hh], bs[hh].ap[R, 0:64], TMbk.ap[:, 0, tl, R], Us[hh].ap, True, False,
                             [TMbk, Us[hh]], inc=False)
                        k.mm(bs[hh], bs[hh].ap[R, 0:64], TMbk.ap[:, 1, tl, R], vt[hh], False, True, [TMbk, TMav])
                    if own:
                        for hh in range(2):
                            R, h = Rs[hh], hs[hh]
                            o = by_.ap[:, (tl * 2 + hh) * 64:(tl * 2 + hh + 1) * 64]
                            k.mm(by_, o, AR.ap[R, tl, 1, :], Sbf[h].ap[R, :], True, False, [AR, Sbf[h]], inc=False)
                            k.mm(by_, o, A1[hh].ap[:, tl, 128:256], Us[hh].ap, False, False, [A1[hh], Us[hh]],
                                 inc=False)
                            k.mm(by_, o, A2[hh].ap[:, tl, 128:256], vt[hh], False, True, [A2[hh], TMav])
                    yield
                    for hh in range(2):
                        R, h = Rs[hh], hs[hh]
                        pcol = bC.ap[R, tl * 128 + 127:tl * 128 + 128]
                        k.stt("dve", Sbf[h].ap[R, :], S32[h].ap[R, :], pcol, bs[hh].ap[R, 0:64], ALU.mult, ALU.add,
                              [S32[h], bC, bs[hh]], [Sbf[h]])
                    for hh in range(2):
                        R, h = Rs[hh], hs[hh]
                        pcol = bC.ap[R, tl * 128 + 127:tl * 128 + 128]
                        k.stt("dve", S32[h].ap[R, :], S32[h].ap[R, :], pcol, bs[hh].ap[R, 0:64], ALU.mult, ALU.add,
                              [S32[h], bC, bs[hh]], [S32[h]])
                    yield
                if own:
                    k.cp("act", ypair.ap, by_.ap, [by_], [ypair])
                    k.reserved.remove(by_)
                    y8 = v3(ypair.ap, 64)
                    k.op("dve", lambda e: e.tensor_reduce(out=st8.ap[:, 0, :], in_=y8, axis=AX.X, op=ALU.add),
                         reads=[ypair], writes=[st8])
                    k.act(ysq.ap, ypair.ap, AF.Square, [ypair], [ysq])
                    k.op("dve", lambda e: e.tensor_reduce(out=st8.ap[:, 1, :], in_=v3(ysq.ap, 64), axis=AX.X,
                                                          op=ALU.add), reads=[ysq], writes=[st8])
                    yield
                    k.ts("dve", st8.ap[:, 2, :], st8.ap[:, 0, :], 1.0 / 64, None, ALU.mult, None, [st8], [st8])
                    k.tt("dve", st8.ap[:, 3, :], st8.ap[:, 2, :], st8.ap[:, 2, :], ALU.mult, [st8], [st8])
                    k.stt("dve", st8.ap[:, 4, :], st8.ap[:, 1, :], 1.0 / 64, st8.ap[:, 3, :], ALU.mult, ALU.subtract,
                          [st8], [st8])
                    k.act(st8.ap[:, 5, :], st8.ap[:, 4, :], AF.Sqrt, [st8, eps_rms], [st8], bias=eps_rms.ap[:, 2:3])
                    k.op("dve", lambda e: e.reciprocal(out=st8.ap[:, 6, :], in_=st8.ap[:, 5, :]), reads=[st8],
                         writes=[st8])
                    k.tt("dve", y8, y8, st8.ap[:, 2, :].unsqueeze(2).to_broadcast([P, 8, 64]), ALU.subtract,
                         [ypair, st8], [ypair])
                    k.tt("dve", y8, y8, st8.ap[:, 6, :].unsqueeze(2).to_broadcast([P, 8, 64]), ALU.mult,
                         [ypair, st8], [ypair])
                    yield
                    b = k.bank()
                    for tl in range(NTL):
                        k.tr(b, b.ap[:, tl * 128:(tl + 1) * 128], ypair.ap[:, tl * 128:(tl + 1) * 128], ident_f.ap,
                             [ypair, ident_f], inc=(tl == 3))
                    bg = k.bank()
                    k.mm(bg, bg.ap, glu.ap[:, hp * 128:(hp + 1) * 128], sxg.ap[:, 0, :], True, False, [glu, sxg])
                    k.mm(bg, bg.ap, glu.ap[:, 1024 + hp * 128:1024 + (hp + 1) * 128], sxg.ap[:, 1, :], False, True,
                         [glu, sxg])
                    k.act(t1.ap, b.ap, AF.Identity, [b, pvec], [t1], bias=pv("lnb", hp), scale=pv("lng", hp))
                    k.tt("dve", t1.ap, t1.ap, bG.ap, ALU.add, [t1, bG], [t1])
                    k.tt("dve", ya_fm.ap[:, hp, :], t1.ap, bg.ap, ALU.mult, [t1, bg], [ya_fm])
            gens = [pair_gen(hp, sets[hp % 2]) for hp in range(8)]

            def _adv(g):
                try:
                    return next(g)
                except StopIteration:
                    return "END"

            curg = gens[0]
            while _adv(curg) not in ("Q", "END"):
                pass
            for hp in range(8):
                nxtg = gens[hp + 1] if hp < 7 else None
                cur_done = False
                nxt_done = nxtg is None
                acc = 0.0
                in_chain = False
                while not (cur_done and nxt_done):
                    if not cur_done:
                        _v = _adv(curg)
                        cur_done = _v == "END"
                        in_chain = in_chain or _v == "C"
                        if in_chain:
                            acc += PQ_RATIO
                    else:
                        acc = 1e9
                    while not nxt_done and acc >= 1.0:
                        nxt_done = _adv(nxtg) in ("Q", "END")
                        acc -= 1.0
                curg = nxtg
            if blk == OWN0:
                dbg_dump("ya_fm", ya_fm)
            k.arena_reset()
            if not own:
                continue

            u_g = A([P, 8, TB], F32)
            v_g = A([P, NTL, 1024], F32)
            v_ln = A([P, NTL, 1024], BF16)
            bst = A([P, 2, 6], F32)
            mv = A([P, 4], F32)
            for i in range(4):
                wv = wload(f"su{i}")
                for sl in range(2):
                    b = k.bank()
                    proj_fm(b, wv, sl * 128, 128)
                    k.act(u_g.ap[:, i * 2 + sl, :], b.ap, AF.Gelu, [b], [u_g])
            for i in range(4):
                slot, view = wload(f"sv{i}")
                for half in range(2):
                    b = k.bank()
                    for u in range(2):
                        tl = half * 2 + u
                        for kc in range(KD):
                            k.mm(b, b.ap[:, u * 256:(u + 1) * 256], n_fm.ap[:, kc, tl * 128:(tl + 1) * 128],
                                 view[:, kc, :], kc == 0, kc == KD - 1, [slot, n_fm], inc=(kc == KD - 1))
                    k.act(v_g.ap[:, half * 2:half * 2 + 2, i * 256:(i + 1) * 256], v3(b.ap, 256), AF.Gelu, [b], [v_g])
            for tl in range(NTL):
                for j in range(2):
                    k.op("dve", lambda e, j=j, tl=tl: e.bn_stats(out=bst.ap[:, j, :],
                                                                in_=v_g.ap[:, tl, j * 512:(j + 1) * 512]),
                         reads=[v_g], writes=[bst])
                k.op("dve", lambda e: e.bn_aggr(out=mv.ap[:, 0:2], in_=bst.ap.rearrange("p a b -> p (a b)")),
                     reads=[bst], writes=[mv])
                k.act(mv.ap[:, 2:3], mv.ap[:, 1:2], AF.Sqrt, [mv, eps_rms], [mv], bias=eps_rms.ap[:, 1:2])
                k.op("dve", lambda e: e.reciprocal(out=mv.ap[:, 3:4], in_=mv.ap[:, 2:3]), reads=[mv], writes=[mv])
                k.ts("dve", v_g.ap[:, tl, :], v_g.ap[:, tl, :], mv.ap[:, 0:1], mv.ap[:, 3:4], ALU.subtract, ALU.mult,
                     [v_g, mv], [v_g])
                k.tt("dve", v_g.ap[:, tl, :], v_g.ap[:, tl, :], bvec.ap[:, 0:1024], ALU.mult, [v_g, bvec], [v_g])
                k.tt("dve", v_ln.ap[:, tl, :], v_g.ap[:, tl, :], bvec.ap[:, 1024:2048], ALU.add, [v_g, bvec], [v_ln])
            for g in range(8):
                b = k.bank()
                for tl in range(NTL):
                    o = b.ap[:, tl * 128:(tl + 1) * 128]
                    k.mm(b, o, ones_row.ap[0:1, :], sgub.ap[0:1, g * 128:(g + 1) * 128], True, False, [ones_row, sgub])
                    k.mm(b, o, v_ln.ap[:, tl, g * 128:(g + 1) * 128], sguw_b.ap[:, g * 128:(g + 1) * 128], False, True,
                         [v_ln, sguw_b], inc=(tl == 3))
                k.tt("dve", yb_fm.ap[:, g, :], u_g.ap[:, g, :], b.ap, ALU.mult, [u_g, b], [yb_fm])
            if blk == OWN0:
                dbg_dump("yb_fm", yb_fm)
            k.arena_reset()

            h_fm = A([P, KD, TB], F32)
            phase_base = k.sb_ptr
            xt = [A([P, D], F32), A([P, D], F32)]
            load_x_fm(blk, h_fm, xt)
            k.barrier()
            k.sb_ptr = phase_base
            merged = A([P, KD, TB], BF16)
            sa = A([P, TB], F32); sb_ = A([P, TB], F32)
            for cg in range(8):
                wga = wload(f"ga{cg}")
                wgb = wload(f"gb{cg}")
                wpj = wload(f"pj{cg}")
                for sl in range(2):
                    s = cg * 2 + sl
                    bga = k.bank(); proj_fm(bga, wga, sl * 128, 128)
                    bgb = k.bank(); proj_fm(bgb, wgb, sl * 128, 128)
                    bpa = k.bank()
                    for kc in range(8):
                        k.mm(bpa, bpa.ap, wpj[1][:, kc, sl * 128:(sl + 1) * 128], ya_fm.ap[:, kc, :], kc == 0, kc == 7,
                             [wpj[0], ya_fm], inc=(kc == 7))
                    bpb = k.bank()
                    for kc in range(8):
                        k.mm(bpb, bpb.ap, wpj[1][:, 8 + kc, sl * 128:(sl + 1) * 128], yb_fm.ap[:, kc, :], kc == 0,
                             kc == 7, [wpj[0], yb_fm], inc=(kc == 7))
                    k.act(sa.ap, bga.ap, AF.Sigmoid, [bga], [sa])
                    k.act(sb_.ap, bgb.ap, AF.Sigmoid, [bgb], [sb_])
                    k.tt("dve", sa.ap, sa.ap, bpa.ap, ALU.mult, [sa, bpa], [sa])
                    k.tt("dve", sb_.ap, sb_.ap, bpb.ap, ALU.mult, [sb_, bpb], [sb_])
                    k.tt("dve", merged.ap[:, s, :], sa.ap, sb_.ap, ALU.add, [sa, sb_], [merged])
            for cg in range(8):
                slot, view = wload(f"wo{cg}")
                for sl in range(2):
                    s = cg * 2 + sl
                    b = k.bank()
                    for kc in range(KD):
                        k.mm(b, b.ap, view[:, kc, sl * 128:(sl + 1) * 128], merged.ap[:, kc, :], kc == 0, kc == KD - 1,
                             [slot, merged], inc=(kc == KD - 1))
                    k.tt("dve", h_fm.ap[:, s, :], h_fm.ap[:, s, :], b.ap, ALU.add, [h_fm, b], [h_fm])
            k.barrier()
            k.sb_ptr = phase_base
            sq = [A([P, TB], BF16), A([P, TB], BF16)]
            rstd = A([P, TB], F32)
            sg = [A([P, TB], F32), A([P, TB], F32)]
            actb = A([P, 22, TB], BF16)
            rmsnorm_fm(h_fm, "gffn", n_fm, sq, rstd)
            for hf in range(2):
                for j in range(22):
                    slot, view = wload(f"gu{hf * 22 + j}")
                    bg = k.bank()
                    for kc in range(KD):
                        k.mm(bg, bg.ap, view[:, kc, 0:128], n_fm.ap[:, kc, :], kc == 0, kc == KD - 1, [slot, n_fm],
                             inc=(kc == KD - 1))
                    bu = k.bank()
                    for kc in range(KD):
                        k.mm(bu, bu.ap, view[:, kc, 128:256], n_fm.ap[:, kc, :], kc == 0, kc == KD - 1, [slot, n_fm],
                             inc=(kc == KD - 1))
                    s_ = sg[j % 2]
                    k.act(s_.ap, bg.ap, AF.Silu, [bg], [s_])
                    k.tt("dve", actb.ap[:, j, :], s_.ap, bu.ap, ALU.mult, [s_, bu], [actb])
                for sp in range(8):
                    slot, view = wload(f"dn{hf}_{sp}")
                    for sl in range(2):
                        s = sp * 2 + sl
                        b = k.bank()
                        for kc in range(22):
                            k.mm(b, b.ap, view[:, kc, sl * 128:(sl + 1) * 128], actb.ap[:, kc, :], kc == 0, kc == 21,
                                 [slot, actb], inc=(kc == 21))
                        k.tt("dve", h_fm.ap[:, s, :], h_fm.ap[:, s, :], b.ap, ALU.add, [h_fm, b], [h_fm])
            rmsnorm_fm(h_fm, "gfin", h_fm, sq, rstd)
            ot = [A([P, D], F32), A([P, D], F32)]
            for tl in range(NTL):
                o = ot[tl % 2]
                for q in range(4):
                    b = k.bank()
                    for j in range(4):
                        kc = q * 4 + j
                        k.tr(b, b.ap[:, j * 128:(j + 1) * 128], h_fm.ap[:, kc, tl * 128:(tl + 1) * 128], ident_f.ap,
                             [h_fm, ident_f], inc=(j == 3))
                    k.cp("act" if q % 2 else "dve", o.ap[:, q * 512:(q + 1) * 512], b.ap, [b], [o])
                r0 = (blk - OWN0) * TB + tl * 128
                k.dma("sp", out_d[r0:r0 + 128, :], o.ap, reads=[o], ds=o_ds[tl % 2])
            k.arena_reset()
        sp = k.engs["sp"]
        for ds in o_ds + [dbg_ds]:
            if ds.count:
                sp.h.wait_ge(ds.sem, ds.count)
    return nc


def prep_inputs(x, norm_mix_g, w_in, shift_mu, w0, w_lora_up, a0, a_lora_up, g_lora_up, k_k, k_a, r_k, lnx_g, lnx_b,
                w_proj_rwkv, sgu_ln_g, sgu_ln_b, sgu_w, sgu_b, w_proj_sgu, w_out, norm_ffn_g, w_ffn_gate, w_ffn_up,
                w_ffn_down, norm_final_g):
    f = lambda a: np.asarray(a, np.float32)
    wst = pack_weights(f(w_in)[0], f(w_proj_rwkv)[0], f(w_proj_sgu)[0], f(w_out)[0], f(w_ffn_gate)[0],
                       f(w_ffn_up)[0], f(w_ffn_down)[0])
    pvec = np.zeros((P, NPV), np.float32)

    def fm(v):
        v = f(v).reshape(-1)
        return v.reshape(-1, P).T

    pvec[:, PV["gmix"]:PV["gmix"] + 16] = fm(norm_mix_g[0])
    pvec[:, PV["gffn"]:PV["gffn"] + 16] = fm(norm_ffn_g[0])
    pvec[:, PV["gfin"]:PV["gfin"] + 16] = fm(norm_final_g)
    mu = f(shift_mu)[0]
    m0 = PV["mu"]
    for hp in range(8):
        pvec[:, m0 + 3 * hp + 0] = mu[1024 + hp * 128:1024 + (hp + 1) * 128]
        pvec[:, m0 + 3 * hp + 1] = mu[2048 + hp * 128:2048 + (hp + 1) * 128]
        pvec[:, m0 + 3 * hp + 2] = mu[hp * 128:(hp + 1) * 128]
    pvec[0:96, m0 + 24] = mu[3072:3168]
    pvec[0:96, m0 + 25] = mu[3168:3264]
    pvec[:, m0 + 26] = mu[3264:3392]
    pvec[:, m0 + 27] = mu[3392:3520]
    for name, arr in (("w0", w0), ("a0", a0), ("kk", k_k), ("ka", k_a), ("rk", r_k), ("lng", lnx_g), ("lnb", lnx_b)):
        pvec[:, PV[name]:PV[name] + 8] = fm(f(arr)[0])
    bvec = np.empty((P, 2048), np.float32)
    bvec[:, 0:1024] = f(sgu_ln_g)[0][None, :]
    bvec[:, 1024:2048] = f(sgu_ln_b)[0][None, :]
    sguw = np.ascontiguousarray(f(sgu_w)[0].transpose(2, 0, 1).reshape(P, 1024))
    sgub = np.ascontiguousarray(f(sgu_b)[0].reshape(1, 1024))
    lora = np.zeros((P, 4096), np.float32)
    lora[0:96, 0:1024] = f(w_lora_up)[0]
    lora[0:96, 1024:2048] = f(a_lora_up)[0]
    gl = f(g_lora_up)[0]
    lora[:, 2048:3072] = gl[0:128]
    lora[:, 3072:4096] = gl[128:256]
    shared = {"wst": wst, "pvec": pvec, "bvec": bvec, "sguw": sguw, "sgub": sgub, "lora": lora}
    xf = f(x)
    in_maps = []
    for c in range(8):
        b, half = c // 2, c % 2
        xs_ = np.zeros((2048, D), np.float32)
        if half == 1:
            xs_[0:1024] = xf[b, 0:1024]
        xs_[1024:2048] = xf[b, half * 1024:(half + 1) * 1024]
        m = dict(shared)
        m["xs"] = xs_
        in_maps.append(m)
    return in_maps


def kernel(**inputs):
    in_maps = prep_inputs(**inputs)
    nc = build_nc()
    res = run_bass_kernel_spmd(nc, in_maps, core_ids=list(range(8)))
    out = np.empty((4, 2048, D), np.float32)
    for c in range(8):
        b, half = c // 2, c % 2
        out[b, half * 1024:(half + 1) * 1024] = res.results[c]["out"]
    return out
```

```python
import math
from contextlib import ExitStack

import numpy as np
import concourse.bass as bass
import concourse.mybir as mybir
from concourse.bass_utils import run_bass_kernel_spmd

F32 = mybir.dt.float32
BF16 = mybir.dt.bfloat16
AF = mybir.ActivationFunctionType
ALU = mybir.AluOpType
AX = mybir.AxisListType

P = 128
D = 2048
KD = 16
TB = 512
NTL = 4
RW = 1024
NH = 16
FH = 5632
NBLK = 4
OWN0 = 2
DEC = -math.exp(-0.5)
RMS_EPS = 1e-6
LN_EPS = 1e-5
LNX_EPS = 64e-5
SLOT = 6144
NSLOT = 4
PQ_RATIO = 0.9

PV = {}
_c = 0
for _n, _w in (("gmix", 16), ("gffn", 16), ("gfin", 16), ("mu", 28), ("w0", 8), ("a0", 8), ("kk", 8),
               ("ka", 8), ("rk", 8), ("lng", 8), ("lnb", 8)):
    PV[_n] = _c
    _c += _w
NPV = _c


def weight_layout():
    lay = {}
    off = 0

    def add(name, kc, nc_):
        nonlocal off
        lay[name] = (off, kc, nc_)
        off += kc * nc_

    for hp in range(8):
        add(f"kvr{hp}", 16, 384)
    add("lora_wa", 16, 192)
    add("lora_g", 16, 256)
    for i in range(4):
        add(f"su{i}", 16, 256)
    for i in range(4):
        add(f"sv{i}", 16, 256)
    for cg in range(8):
        add(f"ga{cg}", 16, 256)
        add(f"gb{cg}", 16, 256)
        add(f"pj{cg}", 16, 256)
    for cg in range(8):
        add(f"wo{cg}", 16, 256)
    for j in range(44):
        add(f"gu{j}", 16, 256)
    for hf in range(2):
        for sp in range(8):
            add(f"dn{hf}_{sp}", 22, 256)
    return lay, off


LAY, WTOT = weight_layout()


def _tile_w(Wm, c0, c1):
    K = Wm.shape[0]
    kc = K // P
    return Wm[:, c0:c1].reshape(kc, P, c1 - c0).transpose(1, 0, 2)


def pack_weights(w_in, w_proj_rwkv, w_proj_sgu, w_out, w_ffn_gate, w_ffn_up, w_ffn_down):
    wst = np.empty((P, WTOT), np.float32)

    def put(name, arr):
        off, kc, nc_ = LAY[name]
        assert arr.shape == (P, kc, nc_), (name, arr.shape)
        wst[:, off:off + kc * nc_] = arr.reshape(P, kc * nc_)

    for hp in range(8):
        a = np.concatenate([_tile_w(w_in, 1024 + hp * 128, 1024 + hp * 128 + 128),
                            _tile_w(w_in, 2048 + hp * 128, 2048 + hp * 128 + 128),
                            _tile_w(w_in, hp * 128, hp * 128 + 128)], axis=2)
        put(f"kvr{hp}", a)
    put("lora_wa", _tile_w(w_in, 3072, 3264))
    put("lora_g", _tile_w(w_in, 3264, 3520))
    for i in range(4):
        put(f"su{i}", _tile_w(w_in, 3520 + i * 256, 3520 + (i + 1) * 256))
        put(f"sv{i}", _tile_w(w_in, 4544 + i * 256, 4544 + (i + 1) * 256))
    for cg in range(8):
        put(f"ga{cg}", _tile_w(w_in, 5568 + cg * 256, 5568 + (cg + 1) * 256))
        put(f"gb{cg}", _tile_w(w_in, 7616 + cg * 256, 7616 + (cg + 1) * 256))
        a = np.concatenate([_tile_w(w_proj_rwkv, cg * 256, (cg + 1) * 256),
                            _tile_w(w_proj_sgu, cg * 256, (cg + 1) * 256)], axis=1)
        put(f"pj{cg}", a)
        put(f"wo{cg}", _tile_w(w_out, cg * 256, (cg + 1) * 256))
    for j in range(44):
        a = np.concatenate([_tile_w(w_ffn_gate, j * 128, (j + 1) * 128),
                            _tile_w(w_ffn_up, j * 128, (j + 1) * 128)], axis=2)
        put(f"gu{j}", a)
    for hf in range(2):
        for sp in range(8):
            put(f"dn{hf}_{sp}", _tile_w(w_ffn_down[hf * 2816:(hf + 1) * 2816], sp * 256, (sp + 1) * 256))
    return wst


class Res:
    __slots__ = ("writer", "readers", "excl")

    def __init__(self, excl=False):
        self.writer = None
        self.readers = {}
        self.excl = excl


class T:
    def __init__(self, h, excl=False):
        self.h = h
        self.ap = h[:]
        self.res = Res(excl)


class Eng:
    def __init__(self, name, h, sem):
        self.name = name
        self.h = h
        self.sem = sem
        self.count = 0
        self.seen = {}


class DSem:
    def __init__(self, key, sem):
        self.key = key
        self.sem = sem
        self.count = 0


class KB:
    def __init__(self, nc, es):
        self.nc = nc
        self.es = es
        self.uid = 0
        self.engs = {}
        for name, h in (("pe", nc.tensor), ("act", nc.scalar), ("dve", nc.vector), ("pool", nc.gpsimd),
                        ("sp", nc.sync)):
            sem = es.enter_context(nc.semaphore(f"sem_{name}"))
            self.engs[name] = Eng(name, h, sem)
        self.dsems = []
        self.banks = [T(es.enter_context(nc.psum_tensor(f"bank{i}", [P, 512], F32)), excl=True) for i in range(8)]
        self.bank_i = 0
        self.reserved = []
        self.sb_ptr = 16640
        self.arena_base = None

    def alloc(self, shape, dtype):
        n = 1
        for s in shape[1:]:
            n *= s
        nbytes = n * (4 if dtype == F32 else 2)
        nbytes = (nbytes + 63) // 64 * 64
        self.uid += 1
        h = self.nc.alloc_sbuf_tensor_at(f"t{self.uid}", list(shape), dtype, offset=self.sb_ptr)
        self.sb_ptr += nbytes
        assert self.sb_ptr <= 229000, self.sb_ptr
        return T(h)

    def arena_reset(self):
        self.barrier()
        self.sb_ptr = self.arena_base

    def new_dsem(self, name):
        self.uid += 1
        d = DSem(f"d{self.uid}", self.es.enter_context(self.nc.semaphore(f"ds_{name}_{self.uid}")))
        self.dsems.append(d)
        return d

    def bank(self, reserve=False):
        while True:
            b = self.banks[self.bank_i]
            self.bank_i = (self.bank_i + 1) % 8
            if b not in self.reserved:
                break
        if reserve:
            self.reserved.append(b)
        return b

    def _wait(self, eng, m, raw):
        key, sem, val = m
        if key == eng.name:
            if eng.name == "pe":
                return
        if eng.seen.get(key, 0) >= val:
            return
        eng.h.wait_ge(sem, val)
        eng.seen[key] = val

    def _pre(self, eng, reads, writes):
        for t in reads:
            r = t.res
            if r.writer is not None:
                self._wait(eng, r.writer, True)
            if r.excl:
                for m in r.readers.values():
                    self._wait(eng, m, False)
        for t in writes:
            r = t.res
            if r.writer is not None:
                self._wait(eng, r.writer, True)
            for m in r.readers.values():
                self._wait(eng, m, False)

    def _post(self, key, mk, reads, writes):
        for t in reads:
            r = t.res
            if r.excl:
                r.writer = mk
                r.readers = {}
            else:
                r.readers[key] = mk
        for t in writes:
            r = t.res
            r.writer = mk
            r.readers = {}

    def op(self, en, fn, reads=(), writes=(), inc=True):
        eng = self.engs[en]
        self._pre(eng, reads, writes)
        ins = fn(eng.h)
        if inc:
            ins.then_inc(eng.sem, 1)
            eng.count += 1
            mk = (en, eng.sem, eng.count)
        else:
            mk = (en, eng.sem, eng.count + 1)
        self._post(en, mk, reads, writes)

    def dma(self, qn, out, in_, reads=(), writes=(), ds=None):
        eng = self.engs[qn]
        self._pre(eng, reads, writes)
        eng.h.dma_start(out=out, in_=in_).then_inc(ds.sem, 16)
        ds.count += 16
        mk = (ds.key, ds.sem, ds.count)
        self._post(ds.key, mk, reads, writes)

    def barrier(self):
        for en in ("pe", "act", "dve", "sp"):
            eng = self.engs[en]
            for on in ("pe", "act", "dve"):
                o = self.engs[on]
                if o.count > 0:
                    self._wait(eng, (on, o.sem, o.count), True)
            for ds in self.dsems:
                if ds.count > 0:
                    self._wait(eng, (ds.key, ds.sem, ds.count), True)

    def mm(self, bank, out, lhsT, rhs, start, stop, reads, inc=None):
        self.op("pe", lambda e: e.matmul(out, lhsT=lhsT, rhs=rhs, start=start, stop=stop), reads=reads,
                writes=[bank], inc=stop if inc is None else inc)

    def tr(self, bank, out, in_, ident, reads, inc):
        self.op("pe", lambda e: e.transpose(out, in_, ident), reads=reads, writes=[bank], inc=inc)

    def act(self, out, in_, func, reads, writes, bias=None, scale=None):
        kw = {}
        if bias is not None:
            kw["bias"] = bias
        if scale is not None:
            kw["scale"] = scale
        self.op("act", lambda e: e.activation(out=out, in_=in_, func=func, **kw), reads=reads, writes=writes)

    def tt(self, en, out, in0, in1, op, reads, writes):
        self.op(en, lambda e: e.tensor_tensor(out=out, in0=in0, in1=in1, op=op), reads=reads, writes=writes)

    def ts(self, en, out, in0, s1, s2, op0, op1, reads, writes):
        if s2 is None:
            self.op(en, lambda e: e.tensor_scalar(out=out, in0=in0, scalar1=s1, scalar2=None, op0=op0),
                    reads=reads, writes=writes)
        else:
            self.op(en, lambda e: e.tensor_scalar(out=out, in0=in0, scalar1=s1, scalar2=s2, op0=op0, op1=op1),
                    reads=reads, writes=writes)

    def stt(self, en, out, in0, scalar, in1, op0, op1, reads, writes):
        self.op(en, lambda e: e.scalar_tensor_tensor(out=out, in0=in0, scalar=scalar, in1=in1, op0=op0, op1=op1),
                reads=reads, writes=writes)

    def cp(self, en, out, in_, reads, writes):
        if en == "act":
            self.act(out, in_, AF.Copy, reads, writes)
        else:
            self.op(en, lambda e: e.tensor_copy(out=out, in_=in_), reads=reads, writes=writes)


def v3(ap, b=128):
    return ap.rearrange("p (a b) -> p a b", b=b)


def build_nc(debug=None):
    nc = bass.Bass("TRN2", target_bir_lowering=False)
    xs = nc.dram_tensor("xs", [2048, D], F32, kind="ExternalInput").ap()
    wst = nc.dram_tensor("wst", [P, WTOT], F32, kind="ExternalInput").ap()
    pvec_d = nc.dram_tensor("pvec", [P, NPV], F32, kind="ExternalInput").ap()
    bvec_d = nc.dram_tensor("bvec", [P, 2048], F32, kind="ExternalInput").ap()
    sguw_d = nc.dram_tensor("sguw", [P, 1024], F32, kind="ExternalInput").ap()
    sgub_d = nc.dram_tensor("sgub", [1, 1024], F32, kind="ExternalInput").ap()
    lora_d = nc.dram_tensor("lora", [P, 4096], F32, kind="ExternalInput").ap()
    out_d = nc.dram_tensor("out", [1024, D], F32, kind="ExternalOutput").ap()
    dbg_d = {}
    if debug:
        for name, shape in debug.items():
            dbg_d[name] = nc.dram_tensor("dbg_" + name, list(shape), F32, kind="ExternalOutput").ap()

    with ExitStack() as es:
        k = KB(nc, es)
        A = k.alloc
        ident_f = A([P, P], F32)
        ident_b = A([P, P], BF16)
        blockones = A([P, P], F32)
        ones_b = A([P, P], BF16)
        ones_row = A([1, P], F32)
        mask512 = A([P, 512], F32)
        maskL = A([P, 512], F32)
        maskU4 = A([P, 512], F32)
        resetm = A([P, 512], F32)
        pvec = A([P, NPV], F32)
        omm = A([P, 28], F32)
        omka = A([P, 8], F32)
        bvec = A([P, 2048], F32)
        sguw_f = A([P, 1024], F32)
        sguw_b = A([P, 1024], BF16)
        sgub = A([1, 1024], F32)
        wlu = A([P, 1024], BF16)
        alu = A([P, 1024], BF16)
        glu = A([P, 2048], BF16)
        pprev = A([P, 28], F32)
        S32 = [A([P, 64], F32) for _ in range(NH)]
        Sbf = [A([P, 64], BF16) for _ in range(NH)]
        ring = [A([P, SLOT], BF16) for _ in range(NSLOT)]
        ring_ds = [k.new_dsem(f"ring{i}") for i in range(NSLOT)]
        k.dsems = []
        n_fm = A([P, KD, TB], BF16)
        ya_fm = A([P, 8, TB], BF16)
        yb_fm = A([P, 8, TB], BF16)
        k.arena_base = k.sb_ptr
        ring_i = [0]
        setup_ds = k.new_dsem("setup")
        setup_ds2 = k.new_dsem("setup2")
        x_ds = [k.new_dsem("x0"), k.new_dsem("x1")]
        o_ds = [k.new_dsem("o0"), k.new_dsem("o1")]
        dbg_ds = k.new_dsem("dbg")

        def pv(name, j=0, rows=P):
            c = PV[name] + j
            return pvec.ap[0:rows, c:c + 1]

        def wload(name, csub=None):
            off, kc, ncol = LAY[name]
            i = ring_i[0]
            ring_i[0] = (i + 1) % NSLOT
            slot = ring[i]
            if csub is None:
                k.dma("pool", slot.ap[:, 0:kc * ncol], wst[:, off:off + kc * ncol], writes=[slot], ds=ring_ds[i])
            else:
                src = wst[:, off:off + kc * ncol].rearrange("p (k n) -> p k n", n=ncol)[:, :, 0:csub]
                dst = slot.ap[:, 0:kc * ncol].rearrange("p (k n) -> p k n", n=ncol)[:, :, 0:csub]
                k.dma("pool", dst, src, writes=[slot], ds=ring_ds[i])
            return slot, slot.ap[:, 0:kc * ncol].rearrange("p (k n) -> p k n", n=ncol)

        def dbg_dump(name, t, ap=None):
            if name in dbg_d:
                k.dma("pool", dbg_d[name], t.ap if ap is None else ap, reads=[t], ds=dbg_ds)

        k.op("pool", lambda e: e.memset(ident_f.ap, 0.0), writes=[ident_f])
        k.op("pool", lambda e: e.affine_select(out=ident_f.ap, in_=ident_f.ap, pattern=[[-1, P]],
                                                 compare_op=ALU.not_equal, fill=1.0, base=0, channel_multiplier=1),
             reads=[ident_f], writes=[ident_f])
        k.cp("dve", ident_b.ap, ident_f.ap, [ident_f], [ident_b])
        k.op("dve", lambda e: e.memset(blockones.ap, 0.0), writes=[blockones])
        k.op("dve", lambda e: e.memset(blockones.ap[0:64, 0:64], 1.0), writes=[blockones])
        k.op("dve", lambda e: e.memset(blockones.ap[64:128, 64:128], 1.0), writes=[blockones])
        k.op("dve", lambda e: e.memset(ones_b.ap, 1.0), writes=[ones_b])
        k.op("dve", lambda e: e.memset(ones_row.ap, 1.0), writes=[ones_row])
        k.op("dve", lambda e: e.memset(pprev.ap, 0.0), writes=[pprev])
        for h in range(NH):
            k.op("dve", lambda e, h=h: e.memset(S32[h].ap, 0.0), writes=[S32[h]])
            k.op("dve", lambda e, h=h: e.memset(Sbf[h].ap, 0.0), writes=[Sbf[h]])
        k.op("pool", lambda e: e.memset(mask512.ap, 1.0), writes=[mask512])
        k.op("pool", lambda e: e.memset(maskL.ap, 1.0), writes=[maskL])
        for u in range(4):
            sl = mask512.ap[:, u * 128:(u + 1) * 128]
            cmpop = ALU.is_gt if u % 2 == 0 else ALU.is_ge
            k.op("pool", lambda e, sl=sl, cmpop=cmpop: e.affine_select(
                out=sl, in_=sl, pattern=[[1, P]], compare_op=cmpop, fill=0.0, base=0, channel_multiplier=-1),
                reads=[mask512], writes=[mask512])
            sl2 = maskL.ap[:, u * 128:(u + 1) * 128]
            k.op("pool", lambda e, sl2=sl2: e.affine_select(
                out=sl2, in_=sl2, pattern=[[-1, P]], compare_op=ALU.is_gt, fill=0.0, base=0, channel_multiplier=1),
                reads=[maskL], writes=[maskL])
        for u in range(4):
            k.cp("dve", maskU4.ap[:, u * 128:(u + 1) * 128], mask512.ap[:, 0:128], [mask512], [maskU4])
        k.op("dve", lambda e: e.memset(resetm.ap, 1.0), writes=[resetm])
        k.op("dve", lambda e: e.memset(v3(resetm.ap)[:, :, 0:1], 0.0), writes=[resetm])
        k.dma("sp", pvec.ap, pvec_d, writes=[pvec], ds=setup_ds)
        k.dma("sp", bvec.ap, bvec_d, writes=[bvec], ds=setup_ds)
        k.dma("sp", sguw_f.ap, sguw_d, writes=[sguw_f], ds=setup_ds)
        k.dma("sp", sgub.ap, sgub_d, writes=[sgub], ds=setup_ds)
        k.dma("pool", wlu.ap[0:96, :], lora_d[0:96, 0:1024], writes=[wlu], ds=setup_ds2)
        k.dma("pool", alu.ap[0:96, :], lora_d[0:96, 1024:2048], writes=[alu], ds=setup_ds2)
        k.dma("pool", glu.ap, lora_d[:, 2048:4096], writes=[glu], ds=setup_ds2)
        _fin = (setup_ds.key, setup_ds.sem, setup_ds.count)
        for _t in (pvec, bvec, sguw_f, sgub):
            _t.res.writer = _fin
        _fin2 = (setup_ds2.key, setup_ds2.sem, setup_ds2.count)
        for _t in (wlu, alu, glu):
            _t.res.writer = _fin2
        mu0 = PV["mu"]
        k.ts("dve", omm.ap, pvec.ap[:, mu0:mu0 + 28], -1.0, 1.0, ALU.mult, ALU.add, [pvec], [omm])
        k.ts("dve", omka.ap, pvec.ap[:, PV["ka"]:PV["ka"] + 8], -1.0, 1.0, ALU.mult, ALU.add, [pvec], [omka])
        k.tt("dve", v3(sguw_b.ap), v3(sguw_f.ap),
             mask512.ap[:, 128:256].unsqueeze(1).to_broadcast([P, 8, 128]), ALU.mult, [sguw_f, mask512], [sguw_b])

        def load_x_fm(blk, h_fm, xt):
            for tl in range(NTL):
                xtt = xt[tl % 2]
                r0 = blk * TB + tl * 128
                k.dma("sp", xtt.ap, xs[r0:r0 + 128, :], writes=[xtt], ds=x_ds[tl % 2])
                for q in range(4):
                    b = k.bank()
                    for j in range(4):
                        kc = q * 4 + j
                        k.tr(b, b.ap[:, j * 128:(j + 1) * 128], xtt.ap[:, kc * 128:(kc + 1) * 128], ident_f.ap,
                             [xtt, ident_f], inc=(j == 3))
                    k.cp("act" if q % 2 else "dve", h_fm.ap[:, q * 4:(q + 1) * 4, tl * 128:(tl + 1) * 128],
                         v3(b.ap), [b], [h_fm])

        def rmsnorm_fm(h_fm, gname, dst, sq, rstd, dst_f32=False):
            b = k.bank()
            for kc in range(KD):
                s = sq[kc % 2]
                k.act(s.ap, h_fm.ap[:, kc, :], AF.Square, [h_fm], [s])
                k.mm(b, b.ap, ones_b.ap, s.ap, kc == 0, kc == KD - 1, [ones_b, s], inc=True)
            k.act(rstd.ap, b.ap, AF.Sqrt, [b, eps_rms], [rstd], bias=eps_rms.ap[:, 0:1], scale=1.0 / D)
            k.op("dve", lambda e: e.reciprocal(out=rstd.ap, in_=rstd.ap), reads=[rstd], writes=[rstd])
            for kc in range(KD):
                k.stt("dve", dst.ap[:, kc, :], h_fm.ap[:, kc, :], pv(gname, kc), rstd.ap, ALU.mult, ALU.mult,
                      [h_fm, pvec, rstd], [dst])

        eps_rms = A([P, 4], F32)
        k.arena_base = k.sb_ptr
        k.op("dve", lambda e: e.memset(eps_rms.ap[:, 0:1], RMS_EPS), writes=[eps_rms])
        k.op("dve", lambda e: e.memset(eps_rms.ap[:, 1:2], LN_EPS), writes=[eps_rms])
        k.op("dve", lambda e: e.memset(eps_rms.ap[:, 2:3], LNX_EPS), writes=[eps_rms])

        def tshift(b, rows, c, dst):
            mu = pvec.ap[0:rows, mu0 + c:mu0 + c + 1]
            k.act(dst.ap[0:rows, :], b.ap[0:rows, :], AF.Identity, [b, omm], [dst], scale=omm.ap[0:rows, c:c + 1])
            k.stt("dve", dst.ap[0:rows, 1:TB], b.ap[0:rows, 0:TB - 1], mu, dst.ap[0:rows, 1:TB], ALU.mult, ALU.add,
                  [b, pvec, dst], [dst])
            k.stt("dve", dst.ap[0:rows, 0:1], pprev.ap[0:rows, c:c + 1], mu, dst.ap[0:rows, 0:1], ALU.mult, ALU.add,
                  [pprev, pvec, dst], [dst])
            k.cp("act", pprev.ap[0:rows, c:c + 1], b.ap[0:rows, TB - 1:TB], [b], [pprev])

        def proj_fm(b, wv, c0, ncol, rows=None):
            slot, view = wv
            for kc in range(KD):
                k.mm(b, b.ap[0:ncol, :], view[:, kc, c0:c0 + ncol], n_fm.ap[:, kc, :], kc == 0, kc == KD - 1,
                     [slot, n_fm], inc=(kc == KD - 1))

        for blk in range(NBLK):
            own = blk >= OWN0
            do_r = blk >= OWN0 - 1
            h0 = A([P, KD, TB], F32)
            xt = [A([P, D], F32), A([P, D], F32)]
            sq = [A([P, TB], BF16), A([P, TB], BF16)]
            rstd = A([P, TB], F32)
            load_x_fm(blk, h0, xt)
            rmsnorm_fm(h0, "gmix", n_fm, sq, rstd)
            if blk == OWN0:
                dbg_dump("n_fm", n_fm)
            k.arena_reset()

            txw = A([P, TB], BF16)
            xab = A([P, TB], BF16)
            sxg = A([P, 2, TB], BF16)
            tmpf = A([P, TB], F32)
            wv = wload("lora_wa")
            b = k.bank()
            proj_fm(b, wv, 0, 96)
            tshift(b, 96, 24, tmpf)
            k.act(txw.ap[0:96, :], tmpf.ap[0:96, :], AF.Tanh, [tmpf], [txw])
            b = k.bank()
            proj_fm(b, wv, 96, 96)
            tshift(b, 96, 25, tmpf)
            k.cp("dve", xab.ap[0:96, :], tmpf.ap[0:96, :], [tmpf], [xab])
            if do_r:
                wv = wload("lora_g")
                for j in range(2):
                    b = k.bank()
                    proj_fm(b, wv, j * 128, 128)
                    tshift(b, 128, 26 + j, tmpf)
                    k.act(sxg.ap[:, j, :], tmpf.ap, AF.Sigmoid, [tmpf], [sxg])

            k_f = A([P, TB], F32); v_f = A([P, TB], F32); r_f = A([P, TB], F32); a_f = A([P, TB], F32)
            bA = A([P, TB], F32); bB = A([P, TB], F32); kkr = A([P, TB], F32)
            bD = A([P, TB], F32); bE = A([P, TB], F32); bF = A([P, TB], F32)
            bh_b = A([P, TB], BF16); kh_b = A([P, TB], BF16); v_b = A([P, TB], BF16)
            A1 = [A([P, NTL, 256], BF16) for _ in range(2)]
            A2 = [A([P, NTL, 256], BF16) for _ in range(2)]
            Xb = [[A([P, NTL, 128], BF16) for _ in range(2)] for _ in range(2)]
            Yb = [[A([P, NTL, 128], BF16) for _ in range(2)] for _ in range(2)]
            Tb = [[A([P, NTL, 128], BF16) for _ in range(2)] for _ in range(2)]
            Xs = [A([P, 64], BF16) for _ in range(2)]
            Us = [A([P, 64], BF16) for _ in range(2)]
            ypair = A([P, TB], F32); ysq = A([P, TB], F32)
            st8 = A([P, 8, 8], F32)
            t1 = A([P, TB], F32)

            def alloc_set():
                bC = A([P, TB], F32); bG = A([P, TB], F32)
                AR = A([P, NTL, 2, 128], BF16)
                bt_b = A([P, TB], BF16); kt_b = A([P, TB], BF16)
                TMav = A([P, 2, NTL, 128], BF16)
                TMbk = A([P, 2, NTL, 128], BF16)
                return (bC, bG, AR, bt_b, kt_b, TMav, TMbk)

            sets = [alloc_set(), alloc_set()]
            if not own:
                for _s in sets:
                    k.op("dve", lambda e, _s=_s: e.memset(_s[2].ap, 0.0), writes=[_s[2]])

            def pair_gen(hp, bs):
                (bC, bG, AR, bt_b, kt_b, TMav, TMbk) = bs
                wv = wload(f"kvr{hp}", None if do_r else 256)
                for ci, dst in enumerate((k_f, v_f, r_f)):
                    if ci == 2 and not do_r:
                        continue
                    b = k.bank()
                    proj_fm(b, wv, ci * 128, 128)
                    tshift(b, 128, 3 * hp + ci, dst)
                    yield
                yield
                b = k.bank()
                k.mm(b, b.ap, wlu.ap[0:96, hp * 128:(hp + 1) * 128], txw.ap[0:96, :], True, True, [wlu, txw])
                yield
                k.act(bA.ap, b.ap, AF.Sigmoid, [b, pvec], [bA], bias=pv("w0", hp))
                yield
                b = k.bank()
                k.mm(b, b.ap, alu.ap[0:96, hp * 128:(hp + 1) * 128], xab.ap[0:96, :], True, True, [alu, xab])
                yield
                k.act(a_f.ap, b.ap, AF.Sigmoid, [b, pvec], [a_f], bias=pv("a0", hp))
                yield
                k.op("dve", lambda e: e.tensor_tensor_scan(out=bB.ap, data0=resetm.ap, data1=bA.ap, initial=0.0,
                                                           op0=ALU.mult, op1=ALU.add), reads=[resetm, bA], writes=[bB])
                k.tt("dve", bA.ap, bB.ap, bA.ap, ALU.subtract, [bB, bA], [bA])
                k.act(bC.ap, bB.ap, AF.Exp, [bB], [bC], scale=DEC)
                k.act(bB.ap, bB.ap, AF.Exp, [bB], [bB], scale=-DEC)
                k.act(bA.ap, bA.ap, AF.Exp, [bA], [bA], scale=DEC)
                yield
                k.ts("dve", kkr.ap, k_f.ap, pv("kk", hp), None, ALU.mult, None, [k_f, pvec], [kkr])
                yield
                k.act(bD.ap, kkr.ap, AF.Square, [kkr], [bD])
                yield
                b = k.bank()
                k.mm(b, b.ap, blockones.ap, bD.ap, True, True, [blockones, bD])
                yield
                k.act(bD.ap, b.ap, AF.Sqrt, [b], [bD])
                yield
                k.ts("dve", bD.ap, bD.ap, 1e-12, None, ALU.max, None, [bD], [bD])
                yield
                k.op("dve", lambda e: e.reciprocal(out=bD.ap, in_=bD.ap), reads=[bD], writes=[bD])
                yield
                k.tt("dve", kkr.ap, kkr.ap, bD.ap, ALU.mult, [kkr, bD], [kkr])
                yield
                k.ts("dve", bE.ap, a_f.ap, pv("ka", hp), omka.ap[:, hp:hp + 1], ALU.mult, ALU.add,
                     [a_f, pvec, omka], [bE])
                k.tt("dve", bE.ap, k_f.ap, bE.ap, ALU.mult, [k_f, bE], [bE])
                if own:
                    k.stt("dve", bG.ap, r_f.ap, pv("rk", hp), bE.ap, ALU.mult, ALU.mult, [r_f, pvec, bE], [bG])
                    yield
                    b = k.bank()
                    k.mm(b, b.ap, blockones.ap, bG.ap, True, True, [blockones, bG])
                    yield
                    k.tt("dve", bG.ap, b.ap, v_f.ap, ALU.mult, [b, v_f], [bG])
                    yield
                yield
                k.tt("dve", bF.ap, kkr.ap, a_f.ap, ALU.mult, [kkr, a_f], [bF])
                k.tt("dve", bF.ap, bF.ap, bB.ap, ALU.mult, [bF, bB], [bF])
                yield
                k.tt("dve", bE.ap, bE.ap, bB.ap, ALU.mult, [bE, bB], [bE])
                yield
                k.stt("dve", AR.ap[:, :, 0, :], v3(kkr.ap), -1.0, v3(bA.ap), ALU.mult, ALU.mult, [kkr, bA], [AR])
                yield
                if own:
                    k.tt("dve", AR.ap[:, :, 1, :], v3(r_f.ap), v3(bC.ap), ALU.mult, [r_f, bC], [AR])
                    yield
                yield
                k.cp("act", bt_b.ap, bF.ap, [bF], [bt_b])
                yield
                k.cp("act", kt_b.ap, bE.ap, [bE], [kt_b])
                yield
                pc = bC.ap[:, 127:TB:128]
                pcb = pc.unsqueeze(2).to_broadcast([P, NTL, 128])
                k.tt("dve", v3(bh_b.ap), v3(bF.ap), pcb, ALU.mult, [bF, bC], [bh_b])
                yield
                k.tt("dve", v3(kh_b.ap), v3(bE.ap), pcb, ALU.mult, [bE, bC], [kh_b])
                yield
                k.cp("act", v_b.ap, v_f.ap, [v_f], [v_b])
                yield
                yield
                for (dstT, srcs) in ((TMav, ((AR, lambda tl: AR.ap[:, tl, 0, :]),
                                              (v_b, lambda tl: v_b.ap[:, tl * 128:(tl + 1) * 128]))),
                                     (TMbk, ((bh_b, lambda tl: bh_b.ap[:, tl * 128:(tl + 1) * 128]),
                                             (kh_b, lambda tl: kh_b.ap[:, tl * 128:(tl + 1) * 128])))):
                    b = k.bank()
                    bb = b.ap.bitcast(BF16)
                    for qi, (srcT, fn) in enumerate(srcs):
                        for tl in range(NTL):
                            o = (qi * NTL + tl) * 128
                            k.tr(b, bb[:, o:o + 128], fn(tl), ident_b.ap, [srcT, ident_b], inc=(qi == 1 and tl == 3))
                    k.cp("act" if dstT is TMav else "dve", dstT.ap.rearrange("p a b c -> p (a b c)"), bb, [b], [dstT])
                    yield
                yield "Q"
                for hh in range(2):
                    R = slice(hh * 64, hh * 64 + 64)
                    if not own:
                        for (src_b, dstA) in ((bt_b, A1[hh]), (kt_b, A2[hh])):
                            b1 = k.bank()
                            for tl in range(NTL):
                                ts_ = slice(tl * 128, tl * 128 + 128)
                                k.mm(b1, b1.ap[:, ts_], src_b.ap[R, ts_], AR.ap[R, tl, 0, :], True, True, [src_b, AR],
                                     inc=(tl == 3))
                            k.tt("dve", dstA.ap[:, :, 0:128], v3(b1.ap), v3(maskU4.ap), ALU.mult, [b1, maskU4], [dstA])
                            yield
                    for half in (range(2) if own else ()):
                        b1 = k.bank()
                        b2 = k.bank()
                        for u in range(2):
                            tl = half * 2 + u
                            ts_ = slice(tl * 128, tl * 128 + 128)
                            rhs = AR.ap[R, tl, :, :].rearrange("p a b -> p (a b)")
                            k.mm(b1, b1.ap[:, u * 256:(u + 1) * 256], bt_b.ap[R, ts_], rhs, True, True, [bt_b, AR],
                                 inc=(u == 1))
                        for u in range(2):
                            tl = half * 2 + u
                            ts_ = slice(tl * 128, tl * 128 + 128)
                            rhs = AR.ap[R, tl, :, :].rearrange("p a b -> p (a b)")
                            k.mm(b2, b2.ap[:, u * 256:(u + 1) * 256], kt_b.ap[R, ts_], rhs, True, True, [kt_b, AR],
                                 inc=(u == 1))
                        k.tt("dve", A1[hh].ap[:, half * 2:half * 2 + 2, :].rearrange("p a b -> p (a b)"), b1.ap,
                             mask512.ap, ALU.mult, [b1, mask512], [A1[hh]])
                        k.tt("dve", A2[hh].ap[:, half * 2:half * 2 + 2, :].rearrange("p a b -> p (a b)"), b2.ap,
                             mask512.ap, ALU.mult, [b2, mask512], [A2[hh]])
                        yield
                    b3 = k.bank()
                    for tl in range(NTL):
                        ts_ = slice(tl * 128, tl * 128 + 128)
                        k.mm(b3, b3.ap[:, ts_], AR.ap[R, tl, 0, :], bt_b.ap[R, ts_], True, True, [AR, bt_b],
                             inc=(tl == 3))
                    k.tt("dve", Yb[hh][0].ap.rearrange("p a b -> p (a b)"), b3.ap, maskL.ap, ALU.mult, [b3, maskL],
                         [Yb[hh][0]])
                    k.cp("act", Xb[hh][0].ap, A1[hh].ap[:, :, 0:128], [A1[hh]], [Xb[hh][0]])
                    k.tt("dve", Tb[hh][0].ap, A1[hh].ap[:, :, 0:128],
                         ident_b.ap.unsqueeze(1).to_broadcast([P, NTL, 128]), ALU.add, [A1[hh], ident_b], [Tb[hh][0]])
                    yield
                cur = 0
                for lvl in range(1, 8):
                    nxt = 1 - cur
                    for hh in range(2):
                        Xc, Yc, Tc = Xb[hh][cur], Yb[hh][cur], Tb[hh][cur]
                        Xn, Yn, Tn = Xb[hh][nxt], Yb[hh][nxt], Tb[hh][nxt]
                        if lvl <= 6:
                            by = k.bank()
                            for tl in range(NTL):
                                k.mm(by, by.ap[:, tl * 128:(tl + 1) * 128], Xc.ap[:, tl, :], Yc.ap[:, tl, :], True, True,
                                     [Xc, Yc], inc=(tl == 3))
                            if lvl <= 5:
                                bx = k.bank()
                                for tl in range(NTL):
                                    k.mm(bx, bx.ap[:, tl * 128:(tl + 1) * 128], Yc.ap[:, tl, :], Xc.ap[:, tl, :], True,
                                         True, [Xc, Yc], inc=(tl == 3))
                        if lvl >= 2:
                            bt = k.bank()
                            for tl in range(NTL):
                                o = bt.ap[:, tl * 128:(tl + 1) * 128]
                                k.mm(bt, o, ident_b.ap, Tc.ap[:, tl, :], True, False, [ident_b, Tc])
                                k.mm(bt, o, Yc.ap[:, tl, :], Tc.ap[:, tl, :], False, True, [Yc, Tc], inc=(tl == 3))
                        if lvl <= 6:
                            k.cp("act", Yn.ap.rearrange("p a b -> p (a b)"), by.ap, [by], [Yn])
                            if lvl <= 5:
                                k.cp("dve", Xn.ap.rearrange("p a b -> p (a b)"), bx.ap, [bx], [Xn])
                        if lvl >= 2:
                            k.cp("dve" if hh else "act", Tn.ap.rearrange("p a b -> p (a b)"), bt.ap, [bt], [Tn])
                        else:
                            k.cp("dve", Tn.ap, Tc.ap, [Tc], [Tn])
                        yield
                    cur = nxt
                Tfin = [Tb[0][cur], Tb[1][cur]]
                by_ = k.bank(reserve=True) if own else None
                for tl in range(NTL):
                    Rs = [slice(0, 64), slice(64, 128)]
                    hs = [hp * 2, hp * 2 + 1]
                    vt = [TMav.ap[:, 1, tl, Rs[hh]] for hh in range(2)]
                    bx = [k.bank(), k.bank()]
                    for hh in range(2):
                        R, h = Rs[hh], hs[hh]
                        k.mm(bx[hh], bx[hh].ap[:, 0:64], AR.ap[R, tl, 0, :], Sbf[h].ap[R, :], True, False,
                             [AR, Sbf[h]], inc=False)
                        k.mm(bx[hh], bx[hh].ap[:, 0:64], A2[hh].ap[:, tl, 0:128], vt[hh], False, True, [A2[hh], TMav])
                    yield
                    k.cp("act", Xs[0].ap, bx[0].ap[:, 0:64], [bx[0]], [Xs[0]])
                    k.cp("dve", Xs[1].ap, bx[1].ap[:, 0:64], [bx[1]], [Xs[1]])
                    yield
                    bu = [k.bank(), k.bank()]
                    for hh in range(2):
                        k.mm(bu[hh], bu[hh].ap[:, 0:64], Tfin[hh].ap[:, tl, :], Xs[hh].ap, True, True,
                             [Tfin[hh], Xs[hh]])
                    yield
                    k.cp("dve", Us[0].ap, bu[0].ap[:, 0:64], [bu[0]], [Us[0]])
                    k.cp("act", Us[1].ap, bu[1].ap[:, 0:64], [bu[1]], [Us[1]])
                    yield
                    bs = [k.bank(), k.bank()]
                    for hh in range(2):
                        R, h = Rs[hh], hs[hh]
                        k.mm(bs[hh], bs[hh].ap[R, 0:64], TMbk.ap[:, 0, tl, R], Us[hh].ap, True, False,
                             [TMbk, Us[hh]], inc=False)
                        k.mm(bs[hh], bs[hh].ap[R, 0:64], TMbk.ap[:, 1, tl, R], vt[hh], False, True, [TMbk, TMav])
                    if own:
                        for hh in range(2):
                            R, h = Rs[hh], hs[hh]
                            o = by_.ap[:, (tl * 2 + hh) * 64:(tl * 2 + hh + 1) * 64]
                            k.mm(by_, o, AR.ap[R, tl, 1, :], Sbf[h].ap[R, :], True, False, [AR, Sbf[h]], inc=False)
                            k.mm(by_, o, A1[hh].ap[:, tl, 128:256], Us[hh].ap, False, False, [A1[hh], Us[hh]],
                                 inc=False)
                            k.mm(by_, o, A2[hh].ap[:, tl, 128:256], vt[hh], False, True, [A2[hh], TMav])
                    yield
                    for hh in range(2):
                        R, h = Rs[hh], hs[hh]
                        pcol = bC.ap[R, tl * 128 + 127:tl * 128 + 128]
                        k.stt("dve", Sbf[h].ap[R, :], S32[h].ap[R, :], pcol, bs[hh].ap[R, 0:64], ALU.mult, ALU.add,
                              [S32[h], bC, bs[hh]], [Sbf[h]])
                    for hh in range(2):
                        R, h = Rs[hh], hs[hh]
                        pcol = bC.ap[R, tl * 128 + 127:tl * 128 + 128]
                        k.stt("dve", S32[h].ap[R, :], S32[h].ap[R, :], pcol, bs[hh].ap[R, 0:64], ALU.mult, ALU.add,
                              [S32[h], bC, bs[hh]], [S32[h]])
                    yield
                if own:
                    k.cp("act", ypair.ap, by_.ap, [by_], [ypair])
                    k.reserved.remove(by_)
                    y8 = v3(ypair.ap, 64)
                    k.op("dve", lambda e: e.tensor_reduce(out=st8.ap[:, 0, :], in_=y8, axis=AX.X, op=ALU.add),
                         reads=[ypair], writes=[st8])
                    k.act(ysq.ap, ypair.ap, AF.Square, [ypair], [ysq])
                    k.op("dve", lambda e: e.tensor_reduce(out=st8.ap[:, 1, :], in_=v3(ysq.ap, 64), axis=AX.X,
                                                          op=ALU.add), reads=[ysq], writes=[st8])
                    yield
                    k.ts("dve", st8.ap[:, 2, :], st8.ap[:, 0, :], 1.0 / 64, None, ALU.mult, None, [st8], [st8])
                    k.tt("dve", st8.ap[:, 3, :], st8.ap[:, 2, :], st8.ap[:, 2, :], ALU.mult, [st8], [st8])
                    k.stt("dve", st8.ap[:, 4, :], st8.ap[:, 1, :], 1.0 / 64, st8.ap[:, 3, :], ALU.mult, ALU.subtract,
                          [st8], [st8])
                    k.act(st8.ap[:, 5, :], st8.ap[:, 4, :], AF.Sqrt, [st8, eps_rms], [st8], bias=eps_rms.ap[:, 2:3])
                    k.op("dve", lambda e: e.reciprocal(out=st8.ap[:, 6, :], in_=st8.ap[:, 5, :]), reads=[st8],
                         writes=[st8])
                    k.tt("dve", y8, y8, st8.ap[:, 2, :].unsqueeze(2).to_broadcast([P, 8, 64]), ALU.subtract,
                         [ypair, st8], [ypair])
                    k.tt("dve", y8, y8, st8.ap[:, 6, :].unsqueeze(2).to_broadcast([P, 8, 64]), ALU.mult,
                         [ypair, st8], [ypair])
                    yield
                    b = k.bank()
                    for tl in range(NTL):
                        k.tr(b, b.ap[:, tl * 128:(tl + 1) * 128], ypair.ap[:, tl * 128:(tl + 1) * 128], ident_f.ap,
                             [ypair, ident_f], inc=(tl == 3))
                    bg = k.bank()
                    k.mm(bg, bg.ap, glu.ap[:, hp * 128:(hp + 1) * 128], sxg.ap[:, 0, :], True, False, [glu, sxg])
                    k.mm(bg, bg.ap, glu.ap[:, 1024 + hp * 128:1024 + (hp + 1) * 128], sxg.ap[:, 1, :], False, True,
                         [glu, sxg])
                    k.act(t1.ap, b.ap, AF.Identity, [b, pvec], [t1], bias=pv("lnb", hp), scale=pv("lng", hp))
                    k.tt("dve", t1.ap, t1.ap, bG.ap, ALU.add, [t1, bG], [t1])
                    k.tt("dve", ya_fm.ap[:, hp, :], t1.ap, bg.ap, ALU.mult, [t1, bg], [ya_fm])
            gens = [pair_gen(hp, sets[hp % 2]) for hp in range(8)]

            def _adv(g):
                try:
                    return next(g)
                except StopIteration:
                    return "END"

            curg = gens[0]
            while _adv(curg) not in ("Q", "END"):
                pass
            for hp in range(8):
                nxtg = gens[hp + 1] if hp < 7 else None
                cur_done = False
                nxt_done = nxtg is None
                acc = 0.0
                while not (cur_done and nxt_done):
                    if not cur_done:
                        cur_done = _adv(curg) == "END"
                        acc += PQ_RATIO
                    else:
                        acc = 1e9
                    while not nxt_done and acc >= 1.0:
                        nxt_done = _adv(nxtg) in ("Q", "END")
                        acc -= 1.0
                curg = nxtg
            if blk == OWN0:
                dbg_dump("ya_fm", ya_fm)
            k.arena_reset()
            if not own:
                continue

            u_g = A([P, 8, TB], F32)
            v_g = A([P, NTL, 1024], F32)
            v_ln = A([P, NTL, 1024], BF16)
            bst = A([P, 2, 6], F32)
            mv = A([P, 4], F32)
            for i in range(4):
                wv = wload(f"su{i}")
                for sl in range(2):
                    b = k.bank()
                    proj_fm(b, wv, sl * 128, 128)
                    k.act(u_g.ap[:, i * 2 + sl, :], b.ap, AF.Gelu, [b], [u_g])
            for i in range(4):
                slot, view = wload(f"sv{i}")
                for half in range(2):
                    b = k.bank()
                    for u in range(2):
                        tl = half * 2 + u
                        for kc in range(KD):
                            k.mm(b, b.ap[:, u * 256:(u + 1) * 256], n_fm.ap[:, kc, tl * 128:(tl + 1) * 128],
                                 view[:, kc, :], kc == 0, kc == KD - 1, [slot, n_fm], inc=(kc == KD - 1))
                    k.act(v_g.ap[:, half * 2:half * 2 + 2, i * 256:(i + 1) * 256], v3(b.ap, 256), AF.Gelu, [b], [v_g])
            for tl in range(NTL):
                for j in range(2):
                    k.op("dve", lambda e, j=j, tl=tl: e.bn_stats(out=bst.ap[:, j, :],
                                                                in_=v_g.ap[:, tl, j * 512:(j + 1) * 512]),
                         reads=[v_g], writes=[bst])
                k.op("dve", lambda e: e.bn_aggr(out=mv.ap[:, 0:2], in_=bst.ap.rearrange("p a b -> p (a b)")),
                     reads=[bst], writes=[mv])
                k.act(mv.ap[:, 2:3], mv.ap[:, 1:2], AF.Sqrt, [mv, eps_rms], [mv], bias=eps_rms.ap[:, 1:2])
                k.op("dve", lambda e: e.reciprocal(out=mv.ap[:, 3:4], in_=mv.ap[:, 2:3]), reads=[mv], writes=[mv])
                k.ts("dve", v_g.ap[:, tl, :], v_g.ap[:, tl, :], mv.ap[:, 0:1], mv.ap[:, 3:4], ALU.subtract, ALU.mult,
                     [v_g, mv], [v_g])
                k.tt("dve", v_g.ap[:, tl, :], v_g.ap[:, tl, :], bvec.ap[:, 0:1024], ALU.mult, [v_g, bvec], [v_g])
                k.tt("dve", v_ln.ap[:, tl, :], v_g.ap[:, tl, :], bvec.ap[:, 1024:2048], ALU.add, [v_g, bvec], [v_ln])
            for g in range(8):
                b = k.bank()
                for tl in range(NTL):
                    o = b.ap[:, tl * 128:(tl + 1) * 128]
                    k.mm(b, o, ones_row.ap[0:1, :], sgub.ap[0:1, g * 128:(g + 1) * 128], True, False, [ones_row, sgub])
                    k.mm(b, o, v_ln.ap[:, tl, g * 128:(g + 1) * 128], sguw_b.ap[:, g * 128:(g + 1) * 128], False, True,
                         [v_ln, sguw_b], inc=(tl == 3))
                k.tt("dve", yb_fm.ap[:, g, :], u_g.ap[:, g, :], b.ap, ALU.mult, [u_g, b], [yb_fm])
            if blk == OWN0:
                dbg_dump("yb_fm", yb_fm)
            k.arena_reset()

            h_fm = A([P, KD, TB], F32)
            phase_base = k.sb_ptr
            xt = [A([P, D], F32), A([P, D], F32)]
            load_x_fm(blk, h_fm, xt)
            k.barrier()
            k.sb_ptr = phase_base
            merged = A([P, KD, TB], BF16)
            sa = A([P, TB], F32); sb_ = A([P, TB], F32)
            for cg in range(8):
                wga = wload(f"ga{cg}")
                wgb = wload(f"gb{cg}")
                wpj = wload(f"pj{cg}")
                for sl in range(2):
                    s = cg * 2 + sl
                    bga = k.bank(); proj_fm(bga, wga, sl * 128, 128)
                    bgb = k.bank(); proj_fm(bgb, wgb, sl * 128, 128)
                    bpa = k.bank()
                    for kc in range(8):
                        k.mm(bpa, bpa.ap, wpj[1][:, kc, sl * 128:(sl + 1) * 128], ya_fm.ap[:, kc, :], kc == 0, kc == 7,
                             [wpj[0], ya_fm], inc=(kc == 7))
                    bpb = k.bank()
                    for kc in range(8):
                        k.mm(bpb, bpb.ap, wpj[1][:, 8 + kc, sl * 128:(sl + 1) * 128], yb_fm.ap[:, kc, :], kc == 0,
                             kc == 7, [wpj[0], yb_fm], inc=(kc == 7))
                    k.act(sa.ap, bga.ap, AF.Sigmoid, [bga], [sa])
                    k.act(sb_.ap, bgb.ap, AF.Sigmoid, [bgb], [sb_])
                    k.tt("dve", sa.ap, sa.ap, bpa.ap, ALU.mult, [sa, bpa], [sa])
                    k.tt("dve", sb_.ap, sb_.ap, bpb.ap, ALU.mult, [sb_, bpb], [sb_])
                    k.tt("dve", merged.ap[:, s, :], sa.ap, sb_.ap, ALU.add, [sa, sb_], [merged])
            for cg in range(8):
                slot, view = wload(f"wo{cg}")
                for sl in range(2):
                    s = cg * 2 + sl
                    b = k.bank()
                    for kc in range(KD):
                        k.mm(b, b.ap, view[:, kc, sl * 128:(sl + 1) * 128], merged.ap[:, kc, :], kc == 0, kc == KD - 1,
                             [slot, merged], inc=(kc == KD - 1))
                    k.tt("dve", h_fm.ap[:, s, :], h_fm.ap[:, s, :], b.ap, ALU.add, [h_fm, b], [h_fm])
            k.barrier()
            k.sb_ptr = phase_base
            sq = [A([P, TB], BF16), A([P, TB], BF16)]
            rstd = A([P, TB], F32)
            sg = [A([P, TB], F32), A([P, TB], F32)]
            actb = A([P, 22, TB], BF16)
            rmsnorm_fm(h_fm, "gffn", n_fm, sq, rstd)
            for hf in range(2):
                for j in range(22):
                    slot, view = wload(f"gu{hf * 22 + j}")
                    bg = k.bank()
                    for kc in range(KD):
                        k.mm(bg, bg.ap, view[:, kc, 0:128], n_fm.ap[:, kc, :], kc == 0, kc == KD - 1, [slot, n_fm],
                             inc=(kc == KD - 1))
                    bu = k.bank()
                    for kc in range(KD):
                        k.mm(bu, bu.ap, view[:, kc, 128:256], n_fm.ap[:, kc, :], kc == 0, kc == KD - 1, [slot, n_fm],
                             inc=(kc == KD - 1))
                    s_ = sg[j % 2]
                    k.act(s_.ap, bg.ap, AF.Silu, [bg], [s_])
                    k.tt("dve", actb.ap[:, j, :], s_.ap, bu.ap, ALU.mult, [s_, bu], [actb])
                for sp in range(8):
                    slot, view = wload(f"dn{hf}_{sp}")
                    for sl in range(2):
                        s = sp * 2 + sl
                        b = k.bank()
                        for kc in range(22):
                            k.mm(b, b.ap, view[:, kc, sl * 128:(sl + 1) * 128], actb.ap[:, kc, :], kc == 0, kc == 21,
                                 [slot, actb], inc=(kc == 21))
                        k.tt("dve", h_fm.ap[:, s, :], h_fm.ap[:, s, :], b.ap, ALU.add, [h_fm, b], [h_fm])
            rmsnorm_fm(h_fm, "gfin", h_fm, sq, rstd)
            ot = [A([P, D], F32), A([P, D], F32)]
            for tl in range(NTL):
                o = ot[tl % 2]
                for q in range(4):
                    b = k.bank()
                    for j in range(4):
                        kc = q * 4 + j
                        k.tr(b, b.ap[:, j * 128:(j + 1) * 128], h_fm.ap[:, kc, tl * 128:(tl + 1) * 128], ident_f.ap,
                             [h_fm, ident_f], inc=(j == 3))
                    k.cp("act" if q % 2 else "dve", o.ap[:, q * 512:(q + 1) * 512], b.ap, [b], [o])
                r0 = (blk - OWN0) * TB + tl * 128
                k.dma("sp", out_d[r0:r0 + 128, :], o.ap, reads=[o], ds=o_ds[tl % 2])
            k.arena_reset()
        sp = k.engs["sp"]
        for ds in o_ds + [dbg_ds]:
            if ds.count:
                sp.h.wait_ge(ds.sem, ds.count)
    return nc


def prep_inputs(x, norm_mix_g, w_in, shift_mu, w0, w_lora_up, a0, a_lora_up, g_lora_up, k_k, k_a, r_k, lnx_g, lnx_b,
                w_proj_rwkv, sgu_ln_g, sgu_ln_b, sgu_w, sgu_b, w_proj_sgu, w_out, norm_ffn_g, w_ffn_gate, w_ffn_up,
                w_ffn_down, norm_final_g):
    f = lambda a: np.asarray(a, np.float32)
    wst = pack_weights(f(w_in)[0], f(w_proj_rwkv)[0], f(w_proj_sgu)[0], f(w_out)[0], f(w_ffn_gate)[0],
                       f(w_ffn_up)[0], f(w_ffn_down)[0])
    pvec = np.zeros((P, NPV), np.float32)

    def fm(v):
        v = f(v).reshape(-1)
        return v.reshape(-1, P).T

    pvec[:, PV["gmix"]:PV["gmix"] + 16] = fm(norm_mix_g[0])
    pvec[:, PV["gffn"]:PV["gffn"] + 16] = fm(norm_ffn_g[0])
    pvec[:, PV["gfin"]:PV["gfin"] + 16] = fm(norm_final_g)
    mu = f(shift_mu)[0]
    m0 = PV["mu"]
    for hp in range(8):
        pvec[:, m0 + 3 * hp + 0] = mu[1024 + hp * 128:1024 + (hp + 1) * 128]
        pvec[:, m0 + 3 * hp + 1] = mu[2048 + hp * 128:2048 + (hp + 1) * 128]
        pvec[:, m0 + 3 * hp + 2] = mu[hp * 128:(hp + 1) * 128]
    pvec[0:96, m0 + 24] = mu[3072:3168]
    pvec[0:96, m0 + 25] = mu[3168:3264]
    pvec[:, m0 + 26] = mu[3264:3392]
    pvec[:, m0 + 27] = mu[3392:3520]
    for name, arr in (("w0", w0), ("a0", a0), ("kk", k_k), ("ka", k_a), ("rk", r_k), ("lng", lnx_g), ("lnb", lnx_b)):
        pvec[:, PV[name]:PV[name] + 8] = fm(f(arr)[0])
    bvec = np.empty((P, 2048), np.float32)
    bvec[:, 0:1024] = f(sgu_ln_g)[0][None, :]
    bvec[:, 1024:2048] = f(sgu_ln_b)[0][None, :]
    sguw = np.ascontiguousarray(f(sgu_w)[0].transpose(2, 0, 1).reshape(P, 1024))
    sgub = np.ascontiguousarray(f(sgu_b)[0].reshape(1, 1024))
    lora = np.zeros((P, 4096), np.float32)
    lora[0:96, 0:1024] = f(w_lora_up)[0]
    lora[0:96, 1024:2048] = f(a_lora_up)[0]
    gl = f(g_lora_up)[0]
    lora[:, 2048:3072] = gl[0:128]
    lora[:, 3072:4096] = gl[128:256]
    shared = {"wst": wst, "pvec": pvec, "bvec": bvec, "sguw": sguw, "sgub": sgub, "lora": lora}
    xf = f(x)
    in_maps = []
    for c in range(8):
        b, half = c // 2, c % 2
        xs_ = np.zeros((2048, D), np.float32)
        if half == 1:
            xs_[0:1024] = xf[b, 0:1024]
        xs_[1024:2048] = xf[b, half * 1024:(half + 1) * 1024]
        m = dict(shared)
        m["xs"] = xs_
        in_maps.append(m)
    return in_maps


def kernel(**inputs):
    in_maps = prep_inputs(**inputs)
    nc = build_nc()
    res = run_bass_kernel_spmd(nc, in_maps, core_ids=list(range(8)))
    out = np.empty((4, 2048, D), np.float32)
    for c in range(8):
        b, half = c // 2, c % 2
        out[b, half * 1024:(half + 1) * 1024] = res.results[c]["out"]
    return out
```

```python
import math
from contextlib import ExitStack

import numpy as np
import concourse.bass as bass
import concourse.mybir as mybir
from concourse.bass_utils import run_bass_kernel_spmd

F32 = mybir.dt.float32
BF16 = mybir.dt.bfloat16
AF = mybir.ActivationFunctionType
ALU = mybir.AluOpType
AX = mybir.AxisListType

P = 128
D = 2048
KD = 16
TB = 512
NTL = 4
RW = 1024
NH = 16
FH = 5632
NBLK = 4
OWN0 = 2
DEC = -math.exp(-0.5)
RMS_EPS = 1e-6
LN_EPS = 1e-5
LNX_EPS = 64e-5
SLOT = 6144
NSLOT = 4
PQ_RATIO = 0.9

PV = {}
_c = 0
for _n, _w in (("gmix", 16), ("gffn", 16), ("gfin", 16), ("mu", 28), ("w0", 8), ("a0", 8), ("kk", 8),
               ("ka", 8), ("rk", 8), ("lng", 8), ("lnb", 8)):
    PV[_n] = _c
    _c += _w
NPV = _c


def weight_layout():
    lay = {}
    off = 0

    def add(name, kc, nc_):
        nonlocal off
        lay[name] = (off, kc, nc_)
        off += kc * nc_

    for hp in range(8):
        add(f"kvr{hp}", 16, 384)
    add("lora_wa", 16, 192)
    add("lora_g", 16, 256)
    for i in range(4):
        add(f"su{i}", 16, 256)
    for i in range(4):
        add(f"sv{i}", 16, 256)
    for cg in range(8):
        add(f"ga{cg}", 16, 256)
        add(f"gb{cg}", 16, 256)
        add(f"pj{cg}", 16, 256)
    for cg in range(8):
        add(f"wo{cg}", 16, 256)
    for j in range(44):
        add(f"gu{j}", 16, 256)
    for hf in range(2):
        for sp in range(8):
            add(f"dn{hf}_{sp}", 22, 256)
    return lay, off


LAY, WTOT = weight_layout()


def _tile_w(Wm, c0, c1):
    K = Wm.shape[0]
    kc = K // P
    return Wm[:, c0:c1].reshape(kc, P, c1 - c0).transpose(1, 0, 2)


def pack_weights(w_in, w_proj_rwkv, w_proj_sgu, w_out, w_ffn_gate, w_ffn_up, w_ffn_down):
    wst = np.empty((P, WTOT), np.float32)

    def put(name, arr):
        off, kc, nc_ = LAY[name]
        assert arr.shape == (P, kc, nc_), (name, arr.shape)
        wst[:, off:off + kc * nc_] = arr.reshape(P, kc * nc_)

    for hp in range(8):
        a = np.concatenate([_tile_w(w_in, 1024 + hp * 128, 1024 + hp * 128 + 128),
                            _tile_w(w_in, 2048 + hp * 128, 2048 + hp * 128 + 128),
                            _tile_w(w_in, hp * 128, hp * 128 + 128)], axis=2)
        put(f"kvr{hp}", a)
    put("lora_wa", _tile_w(w_in, 3072, 3264))
    put("lora_g", _tile_w(w_in, 3264, 3520))
    for i in range(4):
        put(f"su{i}", _tile_w(w_in, 3520 + i * 256, 3520 + (i + 1) * 256))
        put(f"sv{i}", _tile_w(w_in, 4544 + i * 256, 4544 + (i + 1) * 256))
    for cg in range(8):
        put(f"ga{cg}", _tile_w(w_in, 5568 + cg * 256, 5568 + (cg + 1) * 256))
        put(f"gb{cg}", _tile_w(w_in, 7616 + cg * 256, 7616 + (cg + 1) * 256))
        a = np.concatenate([_tile_w(w_proj_rwkv, cg * 256, (cg + 1) * 256),
                            _tile_w(w_proj_sgu, cg * 256, (cg + 1) * 256)], axis=1)
        put(f"pj{cg}", a)
        put(f"wo{cg}", _tile_w(w_out, cg * 256, (cg + 1) * 256))
    for j in range(44):
        a = np.concatenate([_tile_w(w_ffn_gate, j * 128, (j + 1) * 128),
                            _tile_w(w_ffn_up, j * 128, (j + 1) * 128)], axis=2)
        put(f"gu{j}", a)
    for hf in range(2):
        for sp in range(8):
            put(f"dn{hf}_{sp}", _tile_w(w_ffn_down[hf * 2816:(hf + 1) * 2816], sp * 256, (sp + 1) * 256))
    return wst


class Res:
    __slots__ = ("writer", "readers", "excl")

    def __init__(self, excl=False):
        self.writer = None
        self.readers = {}
        self.excl = excl


class T:
    def __init__(self, h, excl=False):
        self.h = h
        self.ap = h[:]
        self.res = Res(excl)


class Eng:
    def __init__(self, name, h, sem):
        self.name = name
        self.h = h
        self.sem = sem
        self.count = 0
        self.seen = {}


class DSem:
    def __init__(self, key, sem):
        self.key = key
        self.sem = sem
        self.count = 0


class KB:
    def __init__(self, nc, es):
        self.nc = nc
        self.es = es
        self.uid = 0
        self.engs = {}
        for name, h in (("pe", nc.tensor), ("act", nc.scalar), ("dve", nc.vector), ("pool", nc.gpsimd),
                        ("sp", nc.sync)):
            sem = es.enter_context(nc.semaphore(f"sem_{name}"))
            self.engs[name] = Eng(name, h, sem)
        self.dsems = []
        self.banks = [T(es.enter_context(nc.psum_tensor(f"bank{i}", [P, 512], F32)), excl=True) for i in range(8)]
        self.bank_i = 0
        self.reserved = []
        self.sb_ptr = 16640
        self.arena_base = None

    def alloc(self, shape, dtype):
        n = 1
        for s in shape[1:]:
            n *= s
        nbytes = n * (4 if dtype == F32 else 2)
        nbytes = (nbytes + 63) // 64 * 64
        self.uid += 1
        h = self.nc.alloc_sbuf_tensor_at(f"t{self.uid}", list(shape), dtype, offset=self.sb_ptr)
        self.sb_ptr += nbytes
        assert self.sb_ptr <= 229000, self.sb_ptr
        return T(h)

    def arena_reset(self):
        self.barrier()
        self.sb_ptr = self.arena_base

    def new_dsem(self, name):
        self.uid += 1
        d = DSem(f"d{self.uid}", self.es.enter_context(self.nc.semaphore(f"ds_{name}_{self.uid}")))
        self.dsems.append(d)
        return d

    def bank(self, reserve=False):
        while True:
            b = self.banks[self.bank_i]
            self.bank_i = (self.bank_i + 1) % 8
            if b not in self.reserved:
                break
        if reserve:
            self.reserved.append(b)
        return b

    def _wait(self, eng, m, raw):
        key, sem, val = m
        if key == eng.name:
            if eng.name == "pe":
                return
        if eng.seen.get(key, 0) >= val:
            return
        eng.h.wait_ge(sem, val)
        eng.seen[key] = val

    def _pre(self, eng, reads, writes):
        for t in reads:
            r = t.res
            if r.writer is not None:
                self._wait(eng, r.writer, True)
            if r.excl:
                for m in r.readers.values():
                    self._wait(eng, m, False)
        for t in writes:
            r = t.res
            if r.writer is not None:
                self._wait(eng, r.writer, True)
            for m in r.readers.values():
                self._wait(eng, m, False)

    def _post(self, key, mk, reads, writes):
        for t in reads:
            r = t.res
            if r.excl:
                r.writer = mk
                r.readers = {}
            else:
                r.readers[key] = mk
        for t in writes:
            r = t.res
            r.writer = mk
            r.readers = {}

    def op(self, en, fn, reads=(), writes=(), inc=True):
        eng = self.engs[en]
        self._pre(eng, reads, writes)
        ins = fn(eng.h)
        if inc:
            ins.then_inc(eng.sem, 1)
            eng.count += 1
            mk = (en, eng.sem, eng.count)
        else:
            mk = (en, eng.sem, eng.count + 1)
        self._post(en, mk, reads, writes)

    def dma(self, qn, out, in_, reads=(), writes=(), ds=None):
        eng = self.engs[qn]
        self._pre(eng, reads, writes)
        eng.h.dma_start(out=out, in_=in_).then_inc(ds.sem, 16)
        ds.count += 16
        mk = (ds.key, ds.sem, ds.count)
        self._post(ds.key, mk, reads, writes)

    def barrier(self):
        for en in ("pe", "act", "dve", "sp"):
            eng = self.engs[en]
            for on in ("pe", "act", "dve"):
                o = self.engs[on]
                if o.count > 0:
                    self._wait(eng, (on, o.sem, o.count), True)
            for ds in self.dsems:
                if ds.count > 0:
                    self._wait(eng, (ds.key, ds.sem, ds.count), True)

    def mm(self, bank, out, lhsT, rhs, start, stop, reads, inc=None):
        self.op("pe", lambda e: e.matmul(out, lhsT=lhsT, rhs=rhs, start=start, stop=stop), reads=reads,
                writes=[bank], inc=stop if inc is None else inc)

    def tr(self, bank, out, in_, ident, reads, inc):
        self.op("pe", lambda e: e.transpose(out, in_, ident), reads=reads, writes=[bank], inc=inc)

    def act(self, out, in_, func, reads, writes, bias=None, scale=None):
        kw = {}
        if bias is not None:
            kw["bias"] = bias
        if scale is not None:
            kw["scale"] = scale
        self.op("act", lambda e: e.activation(out=out, in_=in_, func=func, **kw), reads=reads, writes=writes)

    def tt(self, en, out, in0, in1, op, reads, writes):
        self.op(en, lambda e: e.tensor_tensor(out=out, in0=in0, in1=in1, op=op), reads=reads, writes=writes)

    def ts(self, en, out, in0, s1, s2, op0, op1, reads, writes):
        if s2 is None:
            self.op(en, lambda e: e.tensor_scalar(out=out, in0=in0, scalar1=s1, scalar2=None, op0=op0),
                    reads=reads, writes=writes)
        else:
            self.op(en, lambda e: e.tensor_scalar(out=out, in0=in0, scalar1=s1, scalar2=s2, op0=op0, op1=op1),
                    reads=reads, writes=writes)

    def stt(self, en, out, in0, scalar, in1, op0, op1, reads, writes):
        self.op(en, lambda e: e.scalar_tensor_tensor(out=out, in0=in0, scalar=scalar, in1=in1, op0=op0, op1=op1),
                reads=reads, writes=writes)

    def cp(self, en, out, in_, reads, writes):
        if en == "act":
            self.act(out, in_, AF.Copy, reads, writes)
        else:
            self.op(en, lambda e: e.tensor_copy(out=out, in_=in_), reads=reads, writes=writes)


def v3(ap, b=128):
    return ap.rearrange("p (a b) -> p a b", b=b)


def build_nc(debug=None):
    nc = bass.Bass("TRN2", target_bir_lowering=False)
    xs = nc.dram_tensor("xs", [2048, D], F32, kind="ExternalInput").ap()
    wst = nc.dram_tensor("wst", [P, WTOT], F32, kind="ExternalInput").ap()
    pvec_d = nc.dram_tensor("pvec", [P, NPV], F32, kind="ExternalInput").ap()
    bvec_d = nc.dram_tensor("bvec", [P, 2048], F32, kind="ExternalInput").ap()
    sguw_d = nc.dram_tensor("sguw", [P, 1024], F32, kind="ExternalInput").ap()
    sgub_d = nc.dram_tensor("sgub", [1, 1024], F32, kind="ExternalInput").ap()
    lora_d = nc.dram_tensor("lora", [P, 4096], F32, kind="ExternalInput").ap()
    out_d = nc.dram_tensor("out", [1024, D], F32, kind="ExternalOutput").ap()
    dbg_d = {}
    if debug:
        for name, shape in debug.items():
            dbg_d[name] = nc.dram_tensor("dbg_" + name, list(shape), F32, kind="ExternalOutput").ap()

    with ExitStack() as es:
        k = KB(nc, es)
        A = k.alloc
        ident_f = A([P, P], F32)
        ident_b = A([P, P], BF16)
        blockones = A([P, P], F32)
        ones_b = A([P, P], BF16)
        ones_row = A([1, P], F32)
        mask512 = A([P, 512], F32)
        maskL = A([P, 512], F32)
        maskU4 = A([P, 512], F32)
        resetm = A([P, 512], F32)
        pvec = A([P, NPV], F32)
        omm = A([P, 28], F32)
        omka = A([P, 8], F32)
        bvec = A([P, 2048], F32)
        sguw_f = A([P, 1024], F32)
        sguw_b = A([P, 1024], BF16)
        sgub = A([1, 1024], F32)
        wlu = A([P, 1024], BF16)
        alu = A([P, 1024], BF16)
        glu = A([P, 2048], BF16)
        pprev = A([P, 28], F32)
        S32 = [A([P, 64], F32) for _ in range(NH)]
        Sbf = [A([P, 64], BF16) for _ in range(NH)]
        ring = [A([P, SLOT], BF16) for _ in range(NSLOT)]
        ring_ds = [k.new_dsem(f"ring{i}") for i in range(NSLOT)]
        k.dsems = []
        n_fm = A([P, KD, TB], BF16)
        ya_fm = A([P, 8, TB], BF16)
        yb_fm = A([P, 8, TB], BF16)
        k.arena_base = k.sb_ptr
        ring_i = [0]
        setup_ds = k.new_dsem("setup")
        setup_ds2 = k.new_dsem("setup2")
        x_ds = [k.new_dsem("x0"), k.new_dsem("x1")]
        o_ds = [k.new_dsem("o0"), k.new_dsem("o1")]
        dbg_ds = k.new_dsem("dbg")

        def pv(name, j=0, rows=P):
            c = PV[name] + j
            return pvec.ap[0:rows, c:c + 1]

        def wload(name, csub=None):
            off, kc, ncol = LAY[name]
            i = ring_i[0]
            ring_i[0] = (i + 1) % NSLOT
            slot = ring[i]
            if csub is None:
                k.dma("pool", slot.ap[:, 0:kc * ncol], wst[:, off:off + kc * ncol], writes=[slot], ds=ring_ds[i])
            else:
                src = wst[:, off:off + kc * ncol].rearrange("p (k n) -> p k n", n=ncol)[:, :, 0:csub]
                dst = slot.ap[:, 0:kc * ncol].rearrange("p (k n) -> p k n", n=ncol)[:, :, 0:csub]
                k.dma("pool", dst, src, writes=[slot], ds=ring_ds[i])
            return slot, slot.ap[:, 0:kc * ncol].rearrange("p (k n) -> p k n", n=ncol)

        def dbg_dump(name, t, ap=None):
            if name in dbg_d:
                k.dma("pool", dbg_d[name], t.ap if ap is None else ap, reads=[t], ds=dbg_ds)

        k.op("pool", lambda e: e.memset(ident_f.ap, 0.0), writes=[ident_f])
        k.op("pool", lambda e: e.affine_select(out=ident_f.ap, in_=ident_f.ap, pattern=[[-1, P]],
                                                 compare_op=ALU.not_equal, fill=1.0, base=0, channel_multiplier=1),
             reads=[ident_f], writes=[ident_f])
        k.cp("dve", ident_b.ap, ident_f.ap, [ident_f], [ident_b])
        k.op("dve", lambda e: e.memset(blockones.ap, 0.0), writes=[blockones])
        k.op("dve", lambda e: e.memset(blockones.ap[0:64, 0:64], 1.0), writes=[blockones])
        k.op("dve", lambda e: e.memset(blockones.ap[64:128, 64:128], 1.0), writes=[blockones])
        k.op("dve", lambda e: e.memset(ones_b.ap, 1.0), writes=[ones_b])
        k.op("dve", lambda e: e.memset(ones_row.ap, 1.0), writes=[ones_row])
        k.op("dve", lambda e: e.memset(pprev.ap, 0.0), writes=[pprev])
        for h in range(NH):
            k.op("dve", lambda e, h=h: e.memset(S32[h].ap, 0.0), writes=[S32[h]])
            k.op("dve", lambda e, h=h: e.memset(Sbf[h].ap, 0.0), writes=[Sbf[h]])
        k.op("pool", lambda e: e.memset(mask512.ap, 1.0), writes=[mask512])
        k.op("pool", lambda e: e.memset(maskL.ap, 1.0), writes=[maskL])
        for u in range(4):
            sl = mask512.ap[:, u * 128:(u + 1) * 128]
            cmpop = ALU.is_gt if u % 2 == 0 else ALU.is_ge
            k.op("pool", lambda e, sl=sl, cmpop=cmpop: e.affine_select(
                out=sl, in_=sl, pattern=[[1, P]], compare_op=cmpop, fill=0.0, base=0, channel_multiplier=-1),
                reads=[mask512], writes=[mask512])
            sl2 = maskL.ap[:, u * 128:(u + 1) * 128]
            k.op("pool", lambda e, sl2=sl2: e.affine_select(
                out=sl2, in_=sl2, pattern=[[-1, P]], compare_op=ALU.is_gt, fill=0.0, base=0, channel_multiplier=1),
                reads=[maskL], writes=[maskL])
        for u in range(4):
            k.cp("dve", maskU4.ap[:, u * 128:(u + 1) * 128], mask512.ap[:, 0:128], [mask512], [maskU4])
        k.op("dve", lambda e: e.memset(resetm.ap, 1.0), writes=[resetm])
        k.op("dve", lambda e: e.memset(v3(resetm.ap)[:, :, 0:1], 0.0), writes=[resetm])
        k.dma("sp", pvec.ap, pvec_d, writes=[pvec], ds=setup_ds)
        k.dma("sp", bvec.ap, bvec_d, writes=[bvec], ds=setup_ds)
        k.dma("sp", sguw_f.ap, sguw_d, writes=[sguw_f], ds=setup_ds)
        k.dma("sp", sgub.ap, sgub_d, writes=[sgub], ds=setup_ds)
        k.dma("pool", wlu.ap[0:96, :], lora_d[0:96, 0:1024], writes=[wlu], ds=setup_ds2)
        k.dma("pool", alu.ap[0:96, :], lora_d[0:96, 1024:2048], writes=[alu], ds=setup_ds2)
        k.dma("pool", glu.ap, lora_d[:, 2048:4096], writes=[glu], ds=setup_ds2)
        _fin = (setup_ds.key, setup_ds.sem, setup_ds.count)
        for _t in (pvec, bvec, sguw_f, sgub):
            _t.res.writer = _fin
        _fin2 = (setup_ds2.key, setup_ds2.sem, setup_ds2.count)
        for _t in (wlu, alu, glu):
            _t.res.writer = _fin2
        mu0 = PV["mu"]
        k.ts("dve", omm.ap, pvec.ap[:, mu0:mu0 + 28], -1.0, 1.0, ALU.mult, ALU.add, [pvec], [omm])
        k.ts("dve", omka.ap, pvec.ap[:, PV["ka"]:PV["ka"] + 8], -1.0, 1.0, ALU.mult, ALU.add, [pvec], [omka])
        k.tt("dve", v3(sguw_b.ap), v3(sguw_f.ap),
             mask512.ap[:, 128:256].unsqueeze(1).to_broadcast([P, 8, 128]), ALU.mult, [sguw_f, mask512], [sguw_b])

        def load_x_fm(blk, h_fm, xt):
            for tl in range(NTL):
                xtt = xt[tl % 2]
                r0 = blk * TB + tl * 128
                k.dma("sp", xtt.ap, xs[r0:r0 + 128, :], writes=[xtt], ds=x_ds[tl % 2])
                for q in range(4):
                    b = k.bank()
                    for j in range(4):
                        kc = q * 4 + j
                        k.tr(b, b.ap[:, j * 128:(j + 1) * 128], xtt.ap[:, kc * 128:(kc + 1) * 128], ident_f.ap,
                             [xtt, ident_f], inc=(j == 3))
                    k.cp("act" if q % 2 else "dve", h_fm.ap[:, q * 4:(q + 1) * 4, tl * 128:(tl + 1) * 128],
                         v3(b.ap), [b], [h_fm])

        def rmsnorm_fm(h_fm, gname, dst, sq, rstd, dst_f32=False):
            b = k.bank()
            for kc in range(KD):
                s = sq[kc % 2]
                k.act(s.ap, h_fm.ap[:, kc, :], AF.Square, [h_fm], [s])
                k.mm(b, b.ap, ones_b.ap, s.ap, kc == 0, kc == KD - 1, [ones_b, s], inc=True)
            k.act(rstd.ap, b.ap, AF.Sqrt, [b, eps_rms], [rstd], bias=eps_rms.ap[:, 0:1], scale=1.0 / D)
            k.op("dve", lambda e: e.reciprocal(out=rstd.ap, in_=rstd.ap), reads=[rstd], writes=[rstd])
            for kc in range(KD):
                k.stt("dve", dst.ap[:, kc, :], h_fm.ap[:, kc, :], pv(gname, kc), rstd.ap, ALU.mult, ALU.mult,
                      [h_fm, pvec, rstd], [dst])

        eps_rms = A([P, 4], F32)
        k.arena_base = k.sb_ptr
        k.op("dve", lambda e: e.memset(eps_rms.ap[:, 0:1], RMS_EPS), writes=[eps_rms])
        k.op("dve", lambda e: e.memset(eps_rms.ap[:, 1:2], LN_EPS), writes=[eps_rms])
        k.op("dve", lambda e: e.memset(eps_rms.ap[:, 2:3], LNX_EPS), writes=[eps_rms])

        def tshift(b, rows, c, dst):
            mu = pvec.ap[0:rows, mu0 + c:mu0 + c + 1]
            k.act(dst.ap[0:rows, :], b.ap[0:rows, :], AF.Identity, [b, omm], [dst], scale=omm.ap[0:rows, c:c + 1])
            k.stt("dve", dst.ap[0:rows, 1:TB], b.ap[0:rows, 0:TB - 1], mu, dst.ap[0:rows, 1:TB], ALU.mult, ALU.add,
                  [b, pvec, dst], [dst])
            k.stt("dve", dst.ap[0:rows, 0:1], pprev.ap[0:rows, c:c + 1], mu, dst.ap[0:rows, 0:1], ALU.mult, ALU.add,
                  [pprev, pvec, dst], [dst])
            k.cp("act", pprev.ap[0:rows, c:c + 1], b.ap[0:rows, TB - 1:TB], [b], [pprev])

        def proj_fm(b, wv, c0, ncol, rows=None):
            slot, view = wv
            for kc in range(KD):
                k.mm(b, b.ap[0:ncol, :], view[:, kc, c0:c0 + ncol], n_fm.ap[:, kc, :], kc == 0, kc == KD - 1,
                     [slot, n_fm], inc=(kc == KD - 1))

        for blk in range(NBLK):
            own = blk >= OWN0
            do_r = blk >= OWN0 - 1
            h0 = A([P, KD, TB], F32)
            xt = [A([P, D], F32), A([P, D], F32)]
            sq = [A([P, TB], BF16), A([P, TB], BF16)]
            rstd = A([P, TB], F32)
            load_x_fm(blk, h0, xt)
            rmsnorm_fm(h0, "gmix", n_fm, sq, rstd)
            if blk == OWN0:
                dbg_dump("n_fm", n_fm)
            k.arena_reset()

            txw = A([P, TB], BF16)
            xab = A([P, TB], BF16)
            sxg = A([P, 2, TB], BF16)
            tmpf = A([P, TB], F32)
            wv = wload("lora_wa")
            b = k.bank()
            proj_fm(b, wv, 0, 96)
            tshift(b, 96, 24, tmpf)
            k.act(txw.ap[0:96, :], tmpf.ap[0:96, :], AF.Tanh, [tmpf], [txw])
            b = k.bank()
            proj_fm(b, wv, 96, 96)
            tshift(b, 96, 25, tmpf)
            k.cp("dve", xab.ap[0:96, :], tmpf.ap[0:96, :], [tmpf], [xab])
            if do_r:
                wv = wload("lora_g")
                for j in range(2):
                    b = k.bank()
                    proj_fm(b, wv, j * 128, 128)
                    tshift(b, 128, 26 + j, tmpf)
                    k.act(sxg.ap[:, j, :], tmpf.ap, AF.Sigmoid, [tmpf], [sxg])

            k_f = A([P, TB], F32); v_f = A([P, TB], F32); r_f = A([P, TB], F32); a_f = A([P, TB], F32)
            bA = A([P, TB], F32); bB = A([P, TB], F32); kkr = A([P, TB], F32)
            bD = A([P, TB], F32); bE = A([P, TB], F32); bF = A([P, TB], F32)
            bh_b = A([P, TB], BF16); kh_b = A([P, TB], BF16); v_b = A([P, TB], BF16)
            A1 = [A([P, NTL, 256], BF16) for _ in range(2)]
            A2 = [A([P, NTL, 256], BF16) for _ in range(2)]
            Xb = [[A([P, NTL, 128], BF16) for _ in range(2)] for _ in range(2)]
            Yb = [[A([P, NTL, 128], BF16) for _ in range(2)] for _ in range(2)]
            Tb = [[A([P, NTL, 128], BF16) for _ in range(2)] for _ in range(2)]
            Xs = [A([P, 64], BF16) for _ in range(2)]
            Us = [A([P, 64], BF16) for _ in range(2)]
            ypair = A([P, TB], F32); ysq = A([P, TB], F32)
            st8 = A([P, 8, 8], F32)
            t1 = A([P, TB], F32)

            def alloc_set():
                bC = A([P, TB], F32); bG = A([P, TB], F32)
                AR = A([P, NTL, 2, 128], BF16)
                bt_b = A([P, TB], BF16); kt_b = A([P, TB], BF16)
                TMav = A([P, 2, NTL, 128], BF16)
                TMbk = A([P, 2, NTL, 128], BF16)
                return (bC, bG, AR, bt_b, kt_b, TMav, TMbk)

            sets = [alloc_set(), alloc_set()]
            if not own:
                for _s in sets:
                    k.op("dve", lambda e, _s=_s: e.memset(_s[2].ap, 0.0), writes=[_s[2]])

            def pair_gen(hp, bs):
                (bC, bG, AR, bt_b, kt_b, TMav, TMbk) = bs
                wv = wload(f"kvr{hp}", None if do_r else 256)
                for ci, dst in enumerate((k_f, v_f, r_f)):
                    if ci == 2 and not do_r:
                        continue
                    b = k.bank()
                    proj_fm(b, wv, ci * 128, 128)
                    tshift(b, 128, 3 * hp + ci, dst)
                    yield
                yield
                b = k.bank()
                k.mm(b, b.ap, wlu.ap[0:96, hp * 128:(hp + 1) * 128], txw.ap[0:96, :], True, True, [wlu, txw])
                yield
                k.act(bA.ap, b.ap, AF.Sigmoid, [b, pvec], [bA], bias=pv("w0", hp))
                yield
                b = k.bank()
                k.mm(b, b.ap, alu.ap[0:96, hp * 128:(hp + 1) * 128], xab.ap[0:96, :], True, True, [alu, xab])
                yield
                k.act(a_f.ap, b.ap, AF.Sigmoid, [b, pvec], [a_f], bias=pv("a0", hp))
                yield
                k.op("dve", lambda e: e.tensor_tensor_scan(out=bB.ap, data0=resetm.ap, data1=bA.ap, initial=0.0,
                                                           op0=ALU.mult, op1=ALU.add), reads=[resetm, bA], writes=[bB])
                k.tt("dve", bA.ap, bB.ap, bA.ap, ALU.subtract, [bB, bA], [bA])
                k.act(bC.ap, bB.ap, AF.Exp, [bB], [bC], scale=DEC)
                k.act(bB.ap, bB.ap, AF.Exp, [bB], [bB], scale=-DEC)
                k.act(bA.ap, bA.ap, AF.Exp, [bA], [bA], scale=DEC)
                yield
                k.ts("dve", kkr.ap, k_f.ap, pv("kk", hp), None, ALU.mult, None, [k_f, pvec], [kkr])
                yield
                k.act(bD.ap, kkr.ap, AF.Square, [kkr], [bD])
                yield
                b = k.bank()
                k.mm(b, b.ap, blockones.ap, bD.ap, True, True, [blockones, bD])
                yield
                k.act(bD.ap, b.ap, AF.Sqrt, [b], [bD])
                yield
                k.ts("dve", bD.ap, bD.ap, 1e-12, None, ALU.max, None, [bD], [bD])
                yield
                k.op("dve", lambda e: e.reciprocal(out=bD.ap, in_=bD.ap), reads=[bD], writes=[bD])
                yield
                k.tt("dve", kkr.ap, kkr.ap, bD.ap, ALU.mult, [kkr, bD], [kkr])
                yield
                k.ts("dve", bE.ap, a_f.ap, pv("ka", hp), omka.ap[:, hp:hp + 1], ALU.mult, ALU.add,
                     [a_f, pvec, omka], [bE])
                k.tt("dve", bE.ap, k_f.ap, bE.ap, ALU.mult, [k_f, bE], [bE])
                if own:
                    k.stt("dve", bG.ap, r_f.ap, pv("rk", hp), bE.ap, ALU.mult, ALU.mult, [r_f, pvec, bE], [bG])
                    yield
                    b = k.bank()
                    k.mm(b, b.ap, blockones.ap, bG.ap, True, True, [blockones, bG])
                    yield
                    k.tt("dve", bG.ap, b.ap, v_f.ap, ALU.mult, [b, v_f], [bG])
                    yield
                yield
                k.tt("dve", bF.ap, kkr.ap, a_f.ap, ALU.mult, [kkr, a_f], [bF])
                k.tt("dve", bF.ap, bF.ap, bB.ap, ALU.mult, [bF, bB], [bF])
                yield
                k.tt("dve", bE.ap, bE.ap, bB.ap, ALU.mult, [bE, bB], [bE])
                yield
                k.stt("dve", AR.ap[:, :, 0, :], v3(kkr.ap), -1.0, v3(bA.ap), ALU.mult, ALU.mult, [kkr, bA], [AR])
                yield
                if own:
                    k.tt("dve", AR.ap[:, :, 1, :], v3(r_f.ap), v3(bC.ap), ALU.mult, [r_f, bC], [AR])
                    yield
                yield
                k.cp("act", bt_b.ap, bF.ap, [bF], [bt_b])
                yield
                k.cp("act", kt_b.ap, bE.ap, [bE], [kt_b])
                yield
                pc = bC.ap[:, 127:TB:128]
                pcb = pc.unsqueeze(2).to_broadcast([P, NTL, 128])
                k.tt("dve", v3(bh_b.ap), v3(bF.ap), pcb, ALU.mult, [bF, bC], [bh_b])
                yield
                k.tt("dve", v3(kh_b.ap), v3(bE.ap), pcb, ALU.mult, [bE, bC], [kh_b])
                yield
                k.cp("act", v_b.ap, v_f.ap, [v_f], [v_b])
                yield
                yield
                for (dstT, srcs) in ((TMav, ((AR, lambda tl: AR.ap[:, tl, 0, :]),
                                              (v_b, lambda tl: v_b.ap[:, tl * 128:(tl + 1) * 128]))),
                                     (TMbk, ((bh_b, lambda tl: bh_b.ap[:, tl * 128:(tl + 1) * 128]),
                                             (kh_b, lambda tl: kh_b.ap[:, tl * 128:(tl + 1) * 128])))):
                    b = k.bank()
                    bb = b.ap.bitcast(BF16)
                    for qi, (srcT, fn) in enumerate(srcs):
                        for tl in range(NTL):
                            o = (qi * NTL + tl) * 128
                            k.tr(b, bb[:, o:o + 128], fn(tl), ident_b.ap, [srcT, ident_b], inc=(qi == 1 and tl == 3))
                    k.cp("act" if dstT is TMav else "dve", dstT.ap.rearrange("p a b c -> p (a b c)"), bb, [b], [dstT])
                    yield
                yield "Q"
                for hh in range(2):
                    R = slice(hh * 64, hh * 64 + 64)
                    if not own:
                        for (src_b, dstA) in ((bt_b, A1[hh]), (kt_b, A2[hh])):
                            b1 = k.bank()
                            for tl in range(NTL):
                                ts_ = slice(tl * 128, tl * 128 + 128)
                                k.mm(b1, b1.ap[:, ts_], src_b.ap[R, ts_], AR.ap[R, tl, 0, :], True, True, [src_b, AR],
                                     inc=(tl == 3))
                            k.tt("dve", dstA.ap[:, :, 0:128], v3(b1.ap), v3(maskU4.ap), ALU.mult, [b1, maskU4], [dstA])
                            yield
                    for half in (range(2) if own else ()):
                        b1 = k.bank()
                        b2 = k.bank()
                        for u in range(2):
                            tl = half * 2 + u
                            ts_ = slice(tl * 128, tl * 128 + 128)
                            rhs = AR.ap[R, tl, :, :].rearrange("p a b -> p (a b)")
                            k.mm(b1, b1.ap[:, u * 256:(u + 1) * 256], bt_b.ap[R, ts_], rhs, True, True, [bt_b, AR],
                                 inc=(u == 1))
                        for u in range(2):
                            tl = half * 2 + u
                            ts_ = slice(tl * 128, tl * 128 + 128)
                            rhs = AR.ap[R, tl, :, :].rearrange("p a b -> p (a b)")
                            k.mm(b2, b2.ap[:, u * 256:(u + 1) * 256], kt_b.ap[R, ts_], rhs, True, True, [kt_b, AR],
                                 inc=(u == 1))
                        k.tt("dve", A1[hh].ap[:, half * 2:half * 2 + 2, :].rearrange("p a b -> p (a b)"), b1.ap,
                             mask512.ap, ALU.mult, [b1, mask512], [A1[hh]])
                        k.tt("dve", A2[hh].ap[:, half * 2:half * 2 + 2, :].rearrange("p a b -> p (a b)"), b2.ap,
                             mask512.ap, ALU.mult, [b2, mask512], [A2[hh]])
                        yield
                    b3 = k.bank()
                    for tl in range(NTL):
                        ts_ = slice(tl * 128, tl * 128 + 128)
                        k.mm(b3, b3.ap[:, ts_], AR.ap[R, tl, 0, :], bt_b.ap[R, ts_], True, True, [AR, bt_b],
                             inc=(tl == 3))
                    k.tt("dve", Yb[hh][0].ap.rearrange("p a b -> p (a b)"), b3.ap, maskL.ap, ALU.mult, [b3, maskL],
                         [Yb[hh][0]])
                    k.cp("act", Xb[hh][0].ap, A1[hh].ap[:, :, 0:128], [A1[hh]], [Xb[hh][0]])
                    k.tt("dve", Tb[hh][0].ap, A1[hh].ap[:, :, 0:128],
                         ident_b.ap.unsqueeze(1).to_broadcast([P, NTL, 128]), ALU.add, [A1[hh], ident_b], [Tb[hh][0]])
                    yield
                cur = 0
                tcur = 0
                for lvl in range(1, 8):
                    nxt = 1 - cur
                    for hh in range(2):
                        Xc, Yc, Tc = Xb[hh][cur], Yb[hh][cur], Tb[hh][tcur]
                        Xn, Yn, Tn = Xb[hh][nxt], Yb[hh][nxt], Tb[hh][1 - tcur]
                        if lvl <= 6:
                            by = k.bank()
                            for tl in range(NTL):
                                k.mm(by, by.ap[:, tl * 128:(tl + 1) * 128], Xc.ap[:, tl, :], Yc.ap[:, tl, :], True, True,
                                     [Xc, Yc], inc=(tl == 3))
                            if lvl <= 5:
                                bx = k.bank()
                                for tl in range(NTL):
                                    k.mm(bx, bx.ap[:, tl * 128:(tl + 1) * 128], Yc.ap[:, tl, :], Xc.ap[:, tl, :], True,
                                         True, [Xc, Yc], inc=(tl == 3))
                        if lvl >= 2:
                            bt = k.bank()
                            for tl in range(NTL):
                                o = bt.ap[:, tl * 128:(tl + 1) * 128]
                                k.mm(bt, o, ident_b.ap, Tc.ap[:, tl, :], True, False, [ident_b, Tc])
                                k.mm(bt, o, Yc.ap[:, tl, :], Tc.ap[:, tl, :], False, True, [Yc, Tc], inc=(tl == 3))
                        if lvl <= 6:
                            k.cp("act", Yn.ap.rearrange("p a b -> p (a b)"), by.ap, [by], [Yn])
                            if lvl <= 5:
                                k.cp("dve", Xn.ap.rearrange("p a b -> p (a b)"), bx.ap, [bx], [Xn])
                        if lvl >= 2:
                            k.cp("dve" if hh else "act", Tn.ap.rearrange("p a b -> p (a b)"), bt.ap, [bt], [Tn])
                        yield
                    cur = nxt
                    if lvl >= 2:
                        tcur = 1 - tcur
                Tfin = [Tb[0][tcur], Tb[1][tcur]]
                by_ = k.bank(reserve=True) if own else None
                for tl in range(NTL):
                    Rs = [slice(0, 64), slice(64, 128)]
                    hs = [hp * 2, hp * 2 + 1]
                    vt = [TMav.ap[:, 1, tl, Rs[hh]] for hh in range(2)]
                    bx = [k.bank(), k.bank()]
                    for hh in range(2):
                        R, h = Rs[hh], hs[hh]
                        k.mm(bx[hh], bx[hh].ap[:, 0:64], AR.ap[R, tl, 0, :], Sbf[h].ap[R, :], True, False,
                             [AR, Sbf[h]], inc=False)
                        k.mm(bx[hh], bx[hh].ap[:, 0:64], A2[hh].ap[:, tl, 0:128], vt[hh], False, True, [A2[hh], TMav])
                    yield
                    k.cp("act", Xs[0].ap, bx[0].ap[:, 0:64], [bx[0]], [Xs[0]])
                    k.cp("dve", Xs[1].ap, bx[1].ap[:, 0:64], [bx[1]], [Xs[1]])
                    yield
                    bu = [k.bank(), k.bank()]
                    for hh in range(2):
                        k.mm(bu[hh], bu[hh].ap[:, 0:64], Tfin[hh].ap[:, tl, :], Xs[hh].ap, True, True,
                             [Tfin[hh], Xs[hh]])
                    yield
                    k.cp("dve", Us[0].ap, bu[0].ap[:, 0:64], [bu[0]], [Us[0]])
                    k.cp("act", Us[1].ap, bu[1].ap[:, 0:64], [bu[1]], [Us[1]])
                    yield
                    bs = [k.bank(), k.bank()]
                    for hh in range(2):
                        R, h = Rs[hh], hs[hh]
                        k.mm(bs[hh], bs[hh].ap[R, 0:64], TMbk.ap[:, 0, tl, R], Us[hh].ap, True, False,
                             [TMbk, Us[hh]], inc=False)
                        k.mm(bs[hh], bs[hh].ap[R, 0:64], TMbk.ap[:, 1, tl, R], vt[hh], False, True, [TMbk, TMav])
                    if own:
                        for hh in range(2):
                            R, h = Rs[hh], hs[hh]
                            o = by_.ap[:, (tl * 2 + hh) * 64:(tl * 2 + hh + 1) * 64]
                            k.mm(by_, o, AR.ap[R, tl, 1, :], Sbf[h].ap[R, :], True, False, [AR, Sbf[h]], inc=False)
                            k.mm(by_, o, A1[hh].ap[:, tl, 128:256], Us[hh].ap, False, False, [A1[hh], Us[hh]],
                                 inc=False)
                            k.mm(by_, o, A2[hh].ap[:, tl, 128:256], vt[hh], False, True, [A2[hh], TMav])
                    yield
                    for hh in range(2):
                        R, h = Rs[hh], hs[hh]
                        pcol = bC.ap[R, tl * 128 + 127:tl * 128 + 128]
                        k.stt("dve", Sbf[h].ap[R, :], S32[h].ap[R, :], pcol, bs[hh].ap[R, 0:64], ALU.mult, ALU.add,
                              [S32[h], bC, bs[hh]], [Sbf[h]])
                    for hh in range(2):
                        R, h = Rs[hh], hs[hh]
                        pcol = bC.ap[R, tl * 128 + 127:tl * 128 + 128]
                        k.stt("dve", S32[h].ap[R, :], S32[h].ap[R, :], pcol, bs[hh].ap[R, 0:64], ALU.mult, ALU.add,
                              [S32[h], bC, bs[hh]], [S32[h]])
                    yield
                if own:
                    k.cp("act", ypair.ap, by_.ap, [by_], [ypair])
                    k.reserved.remove(by_)
                    y8 = v3(ypair.ap, 64)
                    k.op("dve", lambda e: e.tensor_reduce(out=st8.ap[:, 0, :], in_=y8, axis=AX.X, op=ALU.add),
                         reads=[ypair], writes=[st8])
                    k.act(ysq.ap, ypair.ap, AF.Square, [ypair], [ysq])
                    k.op("dve", lambda e: e.tensor_reduce(out=st8.ap[:, 1, :], in_=v3(ysq.ap, 64), axis=AX.X,
                                                          op=ALU.add), reads=[ysq], writes=[st8])
                    yield
                    k.ts("dve", st8.ap[:, 2, :], st8.ap[:, 0, :], 1.0 / 64, None, ALU.mult, None, [st8], [st8])
                    k.tt("dve", st8.ap[:, 3, :], st8.ap[:, 2, :], st8.ap[:, 2, :], ALU.mult, [st8], [st8])
                    k.stt("dve", st8.ap[:, 4, :], st8.ap[:, 1, :], 1.0 / 64, st8.ap[:, 3, :], ALU.mult, ALU.subtract,
                          [st8], [st8])
                    k.act(st8.ap[:, 5, :], st8.ap[:, 4, :], AF.Sqrt, [st8, eps_rms], [st8], bias=eps_rms.ap[:, 2:3])
                    k.op("dve", lambda e: e.reciprocal(out=st8.ap[:, 6, :], in_=st8.ap[:, 5, :]), reads=[st8],
                         writes=[st8])
                    k.tt("dve", y8, y8, st8.ap[:, 2, :].unsqueeze(2).to_broadcast([P, 8, 64]), ALU.subtract,
                         [ypair, st8], [ypair])
                    k.tt("dve", y8, y8, st8.ap[:, 6, :].unsqueeze(2).to_broadcast([P, 8, 64]), ALU.mult,
                         [ypair, st8], [ypair])
                    yield
                    b = k.bank()
                    for tl in range(NTL):
                        k.tr(b, b.ap[:, tl * 128:(tl + 1) * 128], ypair.ap[:, tl * 128:(tl + 1) * 128], ident_f.ap,
                             [ypair, ident_f], inc=(tl == 3))
                    bg = k.bank()
                    k.mm(bg, bg.ap, glu.ap[:, hp * 128:(hp + 1) * 128], sxg.ap[:, 0, :], True, False, [glu, sxg])
                    k.mm(bg, bg.ap, glu.ap[:, 1024 + hp * 128:1024 + (hp + 1) * 128], sxg.ap[:, 1, :], False, True,
                         [glu, sxg])
                    k.act(t1.ap, b.ap, AF.Identity, [b, pvec], [t1], bias=pv("lnb", hp), scale=pv("lng", hp))
                    k.tt("dve", t1.ap, t1.ap, bG.ap, ALU.add, [t1, bG], [t1])
                    k.tt("dve", ya_fm.ap[:, hp, :], t1.ap, bg.ap, ALU.mult, [t1, bg], [ya_fm])
            gens = [pair_gen(hp, sets[hp % 2]) for hp in range(8)]

            def _adv(g):
                try:
                    return next(g)
                except StopIteration:
                    return "END"

            curg = gens[0]
            while _adv(curg) not in ("Q", "END"):
                pass
            for hp in range(8):
                nxtg = gens[hp + 1] if hp < 7 else None
                cur_done = False
                nxt_done = nxtg is None
                acc = 0.0
                while not (cur_done and nxt_done):
                    if not cur_done:
                        cur_done = _adv(curg) == "END"
                        acc += PQ_RATIO
                    else:
                        acc = 1e9
                    while not nxt_done and acc >= 1.0:
                        nxt_done = _adv(nxtg) in ("Q", "END")
                        acc -= 1.0
                curg = nxtg
            if blk == OWN0:
                dbg_dump("ya_fm", ya_fm)
            k.arena_reset()
            if not own:
                continue

            u_g = A([P, 8, TB], F32)
            v_g = A([P, NTL, 1024], F32)
            v_ln = A([P, NTL, 1024], BF16)
            bst = A([P, 2, 6], F32)
            mv = A([P, 4], F32)
            for i in range(4):
                wv = wload(f"su{i}")
                for sl in range(2):
                    b = k.bank()
                    proj_fm(b, wv, sl * 128, 128)
                    k.act(u_g.ap[:, i * 2 + sl, :], b.ap, AF.Gelu, [b], [u_g])
            for i in range(4):
                slot, view = wload(f"sv{i}")
                for half in range(2):
                    b = k.bank()
                    for u in range(2):
                        tl = half * 2 + u
                        for kc in range(KD):
                            k.mm(b, b.ap[:, u * 256:(u + 1) * 256], n_fm.ap[:, kc, tl * 128:(tl + 1) * 128],
                                 view[:, kc, :], kc == 0, kc == KD - 1, [slot, n_fm], inc=(kc == KD - 1))
                    k.act(v_g.ap[:, half * 2:half * 2 + 2, i * 256:(i + 1) * 256], v3(b.ap, 256), AF.Gelu, [b], [v_g])
            for tl in range(NTL):
                for j in range(2):
                    k.op("dve", lambda e, j=j, tl=tl: e.bn_stats(out=bst.ap[:, j, :],
                                                                in_=v_g.ap[:, tl, j * 512:(j + 1) * 512]),
                         reads=[v_g], writes=[bst])
                k.op("dve", lambda e: e.bn_aggr(out=mv.ap[:, 0:2], in_=bst.ap.rearrange("p a b -> p (a b)")),
                     reads=[bst], writes=[mv])
                k.act(mv.ap[:, 2:3], mv.ap[:, 1:2], AF.Sqrt, [mv, eps_rms], [mv], bias=eps_rms.ap[:, 1:2])
                k.op("dve", lambda e: e.reciprocal(out=mv.ap[:, 3:4], in_=mv.ap[:, 2:3]), reads=[mv], writes=[mv])
                k.ts("dve", v_g.ap[:, tl, :], v_g.ap[:, tl, :], mv.ap[:, 0:1], mv.ap[:, 3:4], ALU.subtract, ALU.mult,
                     [v_g, mv], [v_g])
                k.tt("dve", v_g.ap[:, tl, :], v_g.ap[:, tl, :], bvec.ap[:, 0:1024], ALU.mult, [v_g, bvec], [v_g])
                k.tt("dve", v_ln.ap[:, tl, :], v_g.ap[:, tl, :], bvec.ap[:, 1024:2048], ALU.add, [v_g, bvec], [v_ln])
            for g in range(8):
                b = k.bank()
                for tl in range(NTL):
                    o = b.ap[:, tl * 128:(tl + 1) * 128]
                    k.mm(b, o, ones_row.ap[0:1, :], sgub.ap[0:1, g * 128:(g + 1) * 128], True, False, [ones_row, sgub])
                    k.mm(b, o, v_ln.ap[:, tl, g * 128:(g + 1) * 128], sguw_b.ap[:, g * 128:(g + 1) * 128], False, True,
                         [v_ln, sguw_b], inc=(tl == 3))
                k.tt("dve", yb_fm.ap[:, g, :], u_g.ap[:, g, :], b.ap, ALU.mult, [u_g, b], [yb_fm])
            if blk == OWN0:
                dbg_dump("yb_fm", yb_fm)
            k.arena_reset()

            h_fm = A([P, KD, TB], F32)
            phase_base = k.sb_ptr
            xt = [A([P, D], F32), A([P, D], F32)]
            load_x_fm(blk, h_fm, xt)
            k.barrier()
            k.sb_ptr = phase_base
            merged = A([P, KD, TB], BF16)
            sa = A([P, TB], F32); sb_ = A([P, TB], F32)
            for cg in range(8):
                wga = wload(f"ga{cg}")
                wgb = wload(f"gb{cg}")
                wpj = wload(f"pj{cg}")
                for sl in range(2):
                    s = cg * 2 + sl
                    bga = k.bank(); proj_fm(bga, wga, sl * 128, 128)
                    bgb = k.bank(); proj_fm(bgb, wgb, sl * 128, 128)
                    bpa = k.bank()
                    for kc in range(8):
                        k.mm(bpa, bpa.ap, wpj[1][:, kc, sl * 128:(sl + 1) * 128], ya_fm.ap[:, kc, :], kc == 0, kc == 7,
                             [wpj[0], ya_fm], inc=(kc == 7))
                    bpb = k.bank()
                    for kc in range(8):
                        k.mm(bpb, bpb.ap, wpj[1][:, 8 + kc, sl * 128:(sl + 1) * 128], yb_fm.ap[:, kc, :], kc == 0,
                             kc == 7, [wpj[0], yb_fm], inc=(kc == 7))
                    k.act(sa.ap, bga.ap, AF.Sigmoid, [bga], [sa])
                    k.act(sb_.ap, bgb.ap, AF.Sigmoid, [bgb], [sb_])
                    k.tt("dve", sa.ap, sa.ap, bpa.ap, ALU.mult, [sa, bpa], [sa])
                    k.tt("dve", sb_.ap, sb_.ap, bpb.ap, ALU.mult, [sb_, bpb], [sb_])
                    k.tt("dve", merged.ap[:, s, :], sa.ap, sb_.ap, ALU.add, [sa, sb_], [merged])
            for cg in range(8):
                slot, view = wload(f"wo{cg}")
                for sl in range(2):
                    s = cg * 2 + sl
                    b = k.bank()
                    for kc in range(KD):
                        k.mm(b, b.ap, view[:, kc, sl * 128:(sl + 1) * 128], merged.ap[:, kc, :], kc == 0, kc == KD - 1,
                             [slot, merged], inc=(kc == KD - 1))
                    k.tt("dve", h_fm.ap[:, s, :], h_fm.ap[:, s, :], b.ap, ALU.add, [h_fm, b], [h_fm])
            k.barrier()
            k.sb_ptr = phase_base
            sq = [A([P, TB], BF16), A([P, TB], BF16)]
            rstd = A([P, TB], F32)
            sg = [A([P, TB], F32), A([P, TB], F32)]
            actb = A([P, 22, TB], BF16)
            rmsnorm_fm(h_fm, "gffn", n_fm, sq, rstd)
            for hf in range(2):
                for j in range(22):
                    slot, view = wload(f"gu{hf * 22 + j}")
                    bg = k.bank()
                    for kc in range(KD):
                        k.mm(bg, bg.ap, view[:, kc, 0:128], n_fm.ap[:, kc, :], kc == 0, kc == KD - 1, [slot, n_fm],
                             inc=(kc == KD - 1))
                    bu = k.bank()
                    for kc in range(KD):
                        k.mm(bu, bu.ap, view[:, kc, 128:256], n_fm.ap[:, kc, :], kc == 0, kc == KD - 1, [slot, n_fm],
                             inc=(kc == KD - 1))
                    s_ = sg[j % 2]
                    k.act(s_.ap, bg.ap, AF.Silu, [bg], [s_])
                    k.tt("dve", actb.ap[:, j, :], s_.ap, bu.ap, ALU.mult, [s_, bu], [actb])
                for sp in range(8):
                    slot, view = wload(f"dn{hf}_{sp}")
                    for sl in range(2):
                        s = sp * 2 + sl
                        b = k.bank()
                        for kc in range(22):
                            k.mm(b, b.ap, view[:, kc, sl * 128:(sl + 1) * 128], actb.ap[:, kc, :], kc == 0, kc == 21,
                                 [slot, actb], inc=(kc == 21))
                        k.tt("dve", h_fm.ap[:, s, :], h_fm.ap[:, s, :], b.ap, ALU.add, [h_fm, b], [h_fm])
            rmsnorm_fm(h_fm, "gfin", h_fm, sq, rstd)
            ot = [A([P, D], F32), A([P, D], F32)]
            for tl in range(NTL):
                o = ot[tl % 2]
                for q in range(4):
                    b = k.bank()
                    for j in range(4):
                        kc = q * 4 + j
                        k.tr(b, b.ap[:, j * 128:(j + 1) * 128], h_fm.ap[:, kc, tl * 128:(tl + 1) * 128], ident_f.ap,
                             [h_fm, ident_f], inc=(j == 3))
                    k.cp("act" if q % 2 else "dve", o.ap[:, q * 512:(q + 1) * 512], b.ap, [b], [o])
                r0 = (blk - OWN0) * TB + tl * 128
                k.dma("sp", out_d[r0:r0 + 128, :], o.ap, reads=[o], ds=o_ds[tl % 2])
            k.arena_reset()
        sp = k.engs["sp"]
        for ds in o_ds + [dbg_ds]:
            if ds.count:
                sp.h.wait_ge(ds.sem, ds.count)
    return nc


def prep_inputs(x, norm_mix_g, w_in, shift_mu, w0, w_lora_up, a0, a_lora_up, g_lora_up, k_k, k_a, r_k, lnx_g, lnx_b,
                w_proj_rwkv, sgu_ln_g, sgu_ln_b, sgu_w, sgu_b, w_proj_sgu, w_out, norm_ffn_g, w_ffn_gate, w_ffn_up,
                w_ffn_down, norm_final_g):
    f = lambda a: np.asarray(a, np.float32)
    wst = pack_weights(f(w_in)[0], f(w_proj_rwkv)[0], f(w_proj_sgu)[0], f(w_out)[0], f(w_ffn_gate)[0],
                       f(w_ffn_up)[0], f(w_ffn_down)[0])
    pvec = np.zeros((P, NPV), np.float32)

    def fm(v):
        v = f(v).reshape(-1)
        return v.reshape(-1, P).T

    pvec[:, PV["gmix"]:PV["gmix"] + 16] = fm(norm_mix_g[0])
    pvec[:, PV["gffn"]:PV["gffn"] + 16] = fm(norm_ffn_g[0])
    pvec[:, PV["gfin"]:PV["gfin"] + 16] = fm(norm_final_g)
    mu = f(shift_mu)[0]
    m0 = PV["mu"]
    for hp in range(8):
        pvec[:, m0 + 3 * hp + 0] = mu[1024 + hp * 128:1024 + (hp + 1) * 128]
        pvec[:, m0 + 3 * hp + 1] = mu[2048 + hp * 128:2048 + (hp + 1) * 128]
        pvec[:, m0 + 3 * hp + 2] = mu[hp * 128:(hp + 1) * 128]
    pvec[0:96, m0 + 24] = mu[3072:3168]
    pvec[0:96, m0 + 25] = mu[3168:3264]
    pvec[:, m0 + 26] = mu[3264:3392]
    pvec[:, m0 + 27] = mu[3392:3520]
    for name, arr in (("w0", w0), ("a0", a0), ("kk", k_k), ("ka", k_a), ("rk", r_k), ("lng", lnx_g), ("lnb", lnx_b)):
        pvec[:, PV[name]:PV[name] + 8] = fm(f(arr)[0])
    bvec = np.empty((P, 2048), np.float32)
    bvec[:, 0:1024] = f(sgu_ln_g)[0][None, :]
    bvec[:, 1024:2048] = f(sgu_ln_b)[0][None, :]
    sguw = np.ascontiguousarray(f(sgu_w)[0].transpose(2, 0, 1).reshape(P, 1024))
    sgub = np.ascontiguousarray(f(sgu_b)[0].reshape(1, 1024))
    lora = np.zeros((P, 4096), np.float32)
    lora[0:96, 0:1024] = f(w_lora_up)[0]
    lora[0:96, 1024:2048] = f(a_lora_up)[0]
    gl = f(g_lora_up)[0]
    lora[:, 2048:3072] = gl[0:128]
    lora[:, 3072:4096] = gl[128:256]
    shared = {"wst": wst, "pvec": pvec, "bvec": bvec, "sguw": sguw, "sgub": sgub, "lora": lora}
    xf = f(x)
    in_maps = []
    for c in range(8):
        b, half = c // 2, c % 2
        xs_ = np.zeros((2048, D), np.float32)
        if half == 1:
            xs_[0:1024] = xf[b, 0:1024]
        xs_[1024:2048] = xf[b, half * 1024:(half + 1) * 1024]
        m = dict(shared)
        m["xs"] = xs_
        in_maps.append(m)
    return in_maps


def kernel(**inputs):
    in_maps = prep_inputs(**inputs)
    nc = build_nc()
    res = run_bass_kernel_spmd(nc, in_maps, core_ids=list(range(8)))
    out = np.empty((4, 2048, D), np.float32)
    for c in range(8):
        b, half = c // 2, c % 2
        out[b, half * 1024:(half + 1) * 1024] = res.results[c]["out"]
    return out
```
